# Optimizing a Trainium2 kernel written in Bass

```python
import jax, jax.numpy as jnp
from jax import lax
import numpy as np

D_MODEL = 1024
BATCH = 16
SEQ = 2048
DEPTH = 2

D_FOURIER = D_MODEL // 2
FOURIER_GROUP = 64
N_FOURIER_GROUPS = D_FOURIER // FOURIER_GROUP
D_SCONV = D_MODEL // 2
SCONV_WIDTH = 3
D_CONF = D_MODEL // 2
CONF_WIDTH = 31
N_BRANCH = 3
D_IN = D_FOURIER + 3 * D_SCONV + 2 * D_CONF
D_FF = ((8 * D_MODEL // 3 + 127) // 128) * 128
N_MOD = 9
EPS = 1e-6

kernel_name = "hybrid_fourier_shortconv_conformer_macaron_adaln"


def rms_norm(x, g):
    xf = x.astype(jnp.float32)
    y = xf * lax.rsqrt(jnp.mean(xf * xf, axis=-1, keepdims=True) + EPS)
    return (y * g.astype(jnp.float32)).astype(x.dtype)


def layer_norm(x, g, b):
    xf = x.astype(jnp.float32)
    mu = jnp.mean(xf, axis=-1, keepdims=True)
    var = jnp.mean(jnp.square(xf - mu), axis=-1, keepdims=True)
    y = (xf - mu) * lax.rsqrt(var + EPS)
    return (y * g.astype(jnp.float32) + b.astype(jnp.float32)).astype(x.dtype)


def modulate(h, shift, scale):
    return h * (1.0 + scale[:, None, :]) + shift[:, None, :]


def swiglu(h, w_gate, w_up, w_down):
    return (jax.nn.silu(h @ w_gate) * (h @ w_up)) @ w_down


def depthwise_conv(x, w):
    return lax.conv_general_dilated(
        x, w[:, None, :], window_strides=(1,), padding="SAME",
        dimension_numbers=("NWC", "WIO", "NWC"), feature_group_count=x.shape[-1])


def fourier_mix(u):
    b, s, _ = u.shape
    ug = u.astype(jnp.float32).reshape(b, s, N_FOURIER_GROUPS, FOURIER_GROUP)
    f = jnp.fft.fft2(ug, axes=(1, 3), norm="ortho").real
    return f.reshape(b, s, D_FOURIER).astype(u.dtype)


def token_mixing(h, w_in, conv_short_w, conv_conf_w, conv_conf_b, conf_ln_g, conf_ln_b,
                 w_branch_f, w_branch_s, w_branch_c, w_gate, b_gate, w_out):
    b, s, d = h.shape
    u = h @ w_in
    cuts = np.cumsum([D_FOURIER, D_SCONV, D_SCONV, D_SCONV, D_CONF]).tolist()
    u_f, u_bg, u_cg, u_x, u_ga, u_gb = jnp.split(u, cuts, axis=-1)
    y_f = fourier_mix(u_f) @ w_branch_f
    y_s = (u_bg * depthwise_conv(u_cg * u_x, conv_short_w)) @ w_branch_s
    v = u_ga * jax.nn.sigmoid(u_gb)
    v = depthwise_conv(v, conv_conf_w) + conv_conf_b
    v = jax.nn.silu(layer_norm(v, conf_ln_g, conf_ln_b))
    y_c = v @ w_branch_c
    g = jax.nn.sigmoid(h @ w_gate + b_gate).reshape(b, s, N_BRANCH, d)
    merged = g[:, :, 0] * y_f + g[:, :, 1] * y_s + g[:, :, 2] * y_c
    return merged @ w_out


def setup_inputs(seed: int = 0) -> dict:
    key = jax.random.key(seed)
    ks = iter(jax.random.split(key, 40))

    def nrm(shape, scale):
        return jax.random.normal(next(ks), shape, jnp.float32) * scale

    def gain(shape):
        return 1.0 + nrm(shape, 0.02)

    L, D = DEPTH, D_MODEL
    return {
        "x": nrm((BATCH, SEQ, D), 1.0),
        "c": nrm((BATCH, D), 1.0),
        "ffn1_norm_g": gain((L, D)),
        "ffn1_w_gate": nrm((L, D, D_FF), D ** -0.5),
        "ffn1_w_up": nrm((L, D, D_FF), D ** -0.5),
        "ffn1_w_down": nrm((L, D_FF, D), D_FF ** -0.5),
        "mix_norm_g": gain((L, D)),
        "w_in": nrm((L, D, D_IN), D ** -0.5),
        "conv_short_w": nrm((L, SCONV_WIDTH, D_SCONV), SCONV_WIDTH ** -0.5),
        "conv_conf_w": nrm((L, CONF_WIDTH, D_CONF), CONF_WIDTH ** -0.5),
        "conv_conf_b": nrm((L, D_CONF), 0.02),
        "conf_ln_g": gain((L, D_CONF)),
        "conf_ln_b": nrm((L, D_CONF), 0.02),
        "w_branch_f": nrm((L, D_FOURIER, D), D_FOURIER ** -0.5),
        "w_branch_s": nrm((L, D_SCONV, D), D_SCONV ** -0.5),
        "w_branch_c": nrm((L, D_CONF, D), D_CONF ** -0.5),
        "w_gate": nrm((L, D, N_BRANCH * D), D ** -0.5),
        "b_gate": nrm((L, N_BRANCH * D), 0.02),
        "w_out": nrm((L, D, D), D ** -0.5),
        "ffn2_norm_g": gain((L, D)),
        "ffn2_w_gate": nrm((L, D, D_FF), D ** -0.5),
        "ffn2_w_up": nrm((L, D, D_FF), D ** -0.5),
        "ffn2_w_down": nrm((L, D_FF, D), D_FF ** -0.5),
        "w_mod": nrm((L, D, N_MOD * D), 0.5 * D ** -0.5),
        "b_mod": nrm((L, N_MOD * D), 0.02),
        "final_norm_g": gain((D,)),
        "w_final_mod": nrm((D, 2 * D), 0.5 * D ** -0.5),
        "b_final_mod": nrm((2 * D,), 0.02),
    }


def reference(x, c, ffn1_norm_g, ffn1_w_gate, ffn1_w_up, ffn1_w_down, mix_norm_g, w_in,
              conv_short_w, conv_conf_w, conv_conf_b, conf_ln_g, conf_ln_b,
              w_branch_f, w_branch_s, w_branch_c, w_gate, b_gate, w_out,
              ffn2_norm_g, ffn2_w_gate, ffn2_w_up, ffn2_w_down, w_mod, b_mod,
              final_norm_g, w_final_mod, b_final_mod):
    c_act = jax.nn.silu(c)
    for l in range(DEPTH):
        mod = c_act @ w_mod[l] + b_mod[l]
        (sh1, sc1, g1, sh2, sc2, g2, sh3, sc3, g3) = jnp.split(mod, N_MOD, axis=-1)
        h = modulate(rms_norm(x, ffn1_norm_g[l]), sh1, sc1)
        x = x + 0.5 * g1[:, None, :] * swiglu(h, ffn1_w_gate[l], ffn1_w_up[l], ffn1_w_down[l])
        h = modulate(rms_norm(x, mix_norm_g[l]), sh2, sc2)
        y = token_mixing(h, w_in[l], conv_short_w[l], conv_conf_w[l], conv_conf_b[l],
                         conf_ln_g[l], conf_ln_b[l], w_branch_f[l], w_branch_s[l],
                         w_branch_c[l], w_gate[l], b_gate[l], w_out[l])
        x = x + g2[:, None, :] * y
        h = modulate(rms_norm(x, ffn2_norm_g[l]), sh3, sc3)
        x = x + 0.5 * g3[:, None, :] * swiglu(h, ffn2_w_gate[l], ffn2_w_up[l], ffn2_w_down[l])
    fmod = c_act @ w_final_mod + b_final_mod
    f_shift, f_scale = jnp.split(fmod, 2, axis=-1)
    return modulate(rms_norm(x, final_norm_g), f_shift, f_scale)
```

```python
import contextlib
import numpy as np
import ml_dtypes
import concourse.bass as bass
import concourse.mybir as mybir
from concourse.bass_utils import run_bass_kernel_spmd
from concourse.ap import AP

F32 = mybir.dt.float32
BF16 = mybir.dt.bfloat16
U8 = mybir.dt.uint8
AF = mybir.ActivationFunctionType
ALU = mybir.AluOpType

D = 1024
T = 2048
NB = 2
DEPTH = 2
DFF = 2816
NFC = 22
DC = 8
TT = 4
TW = 512
EPS = 1e-6
FSPLIT = (8, 8, 6)
NMT = 18 * DEPTH + 4
NKT = 4
KTW = 256

ENGS = ("pe", "act", "dve", "pool", "sp")
BLK = 256


class View:
    __slots__ = ("ap", "keys", "gen")

    def __init__(self, ap, keys, gen=None):
        self.ap = ap
        self.keys = keys
        self.gen = gen

    def sub(self, ap):
        return View(ap, self.keys, self.gen)


class Op:
    __slots__ = ("eng", "fn", "reads", "writes", "is_dma", "eidx", "waits",
                 "signal", "sem", "val", "prewait")

    def __init__(self, eng, fn, reads, writes, is_dma):
        self.eng = eng
        self.fn = fn
        self.reads = reads
        self.writes = writes
        self.is_dma = is_dma
        self.waits = []
        self.signal = False
        self.sem = None
        self.val = 0
        self.prewait = None


class Sched:
    def __init__(self, dma_ring=6):
        self.ops = []
        self.eng_ops = {e: [] for e in ENGS}
        self.K = dma_ring
        self.ps_cur = {}

    def add(self, eng, fn, reads=(), writes=(), dma=False):
        rk = []
        for v in reads:
            rk.extend(v.keys)
            if v.gen is not None:
                assert self.ps_cur[v.gen % 8] == v.gen, "PSUM bank recycled before its reader"
        wk = []
        for v in writes:
            wk.extend(v.keys)
        op = Op(eng, fn, rk, wk, dma)
        op.eidx = len(self.eng_ops[eng])
        self.eng_ops[eng].append(op)
        self.ops.append(op)
        return op

    def analyze(self):
        last_writer = {}
        readers = {}
        known = {e: {} for e in ENGS}
        known_dma = {e: set() for e in ENGS}
        for op in self.ops:
            deps = {}
            for k in op.reads:
                lw = last_writer.get(k)
                if lw is not None and lw is not op:
                    deps[id(lw)] = (lw, True)
            for k in op.writes:
                lw = last_writer.get(k)
                if lw is not None and lw is not op and id(lw) not in deps:
                    deps[id(lw)] = (lw, False)
                rl = readers.get(k)
                if rl:
                    for rd in rl:
                        if rd is not op and id(rd) not in deps:
                            deps[id(rd)] = (rd, False)
            best = {}
            for d, is_raw in deps.values():
                if (not d.is_dma) and (not op.is_dma) and d.eng == op.eng:
                    if d.eng == "pe":
                        continue
                if d.is_dma:
                    if id(d) in known_dma[op.eng]:
                        continue
                    known_dma[op.eng].add(id(d))
                    d.signal = True
                    op.waits.append(d)
                else:
                    cur = best.get(d.eng)
                    if cur is None or d.eidx > cur.eidx:
                        best[d.eng] = d
            for d in best.values():
                if known[op.eng].get(d.eng, -1) >= d.eidx:
                    continue
                known[op.eng][d.eng] = d.eidx
                d.signal = True
                op.waits.append(d)
            for k in op.reads:
                rl = readers.get(k)
                if rl is None:
                    readers[k] = [op]
                elif not rl or rl[-1] is not op:
                    rl.append(op)
            for k in op.writes:
                last_writer[k] = op
                readers[k] = []

    def emit(self, nc, final_wait_ops=()):
        with contextlib.ExitStack() as st:
            esem = {e: st.enter_context(nc.semaphore("s_" + e)) for e in ENGS}
            dsem = {}
            for e in ENGS:
                if any(o.is_dma for o in self.eng_ops[e]):
                    dsem[e] = [st.enter_context(nc.semaphore("d_%s%d" % (e, i)))
                               for i in range(self.K)]
            for e in ENGS:
                cnt = 0
                j = 0
                for op in self.eng_ops[e]:
                    if op.is_dma:
                        op.sem = dsem[e][j % self.K]
                        op.val = 16 * (j // self.K + 1)
                        if j >= self.K:
                            op.prewait = (op.sem, 16 * (j // self.K))
                        j += 1
                    elif op.signal:
                        cnt += 1
                        op.sem = esem[e]
                        op.val = cnt
            block = st.enter_context(nc.Block())

            def make(e):
                def body(eng):
                    for op in self.eng_ops[e]:
                        if op.prewait is not None:
                            eng.wait_ge(op.prewait[0], op.prewait[1])
                        for d in op.waits:
                            eng.wait_ge(d.sem, d.val)
                        if op.fn is None:
                            continue
                        ins = op.fn(eng)
                        if op.is_dma:
                            ins.then_inc(op.sem, 16)
                        elif op.signal:
                            ins.then_inc(op.sem, 1)
                    if e == "sp":
                        for o in final_wait_ops:
                            eng.wait_ge(o.sem, o.val)
                return body

            block.tensor(make("pe"))
            block.scalar(make("act"))
            block.vector(make("dve"))
            block.gpsimd(make("pool"))
            block.sync(make("sp"))


class Arena:
    def __init__(self, nbytes):
        self.nbytes = nbytes
        self.t = None

    def view(self, off, dtype, ncols, shape=None):
        esz = 4 if dtype == F32 else 2
        nb = ncols * esz
        assert off % esz == 0 and off >= 0 and off + nb <= self.nbytes, (off, nb, self.nbytes)
        ap = self.t[:, off:off + nb].bitcast(dtype)
        if shape is not None:
            if len(shape) == 2:
                ap = ap.rearrange("p (a b) -> p a b", a=shape[0], b=shape[1])
            elif len(shape) == 3:
                ap = ap.rearrange("p (a b c) -> p a b c", a=shape[0], b=shape[1], c=shape[2])
        return View(ap, range(off // BLK, (off + nb - 1) // BLK + 1))


def _vec_layout():
    off = {}
    o = 0
    for l in range(DEPTH):
        for name, n in (("g0", 8), ("g1", 8), ("g2", 8), ("bmod", 72), ("csw", 12),
                        ("ccw", 124), ("ccb", 4), ("lng", 4), ("lnb", 4), ("bgate", 24)):
            off[(name, l)] = o
            o += n
    off[("gf", 0)] = o
    o += 8
    off[("bfin", 0)] = o
    o += 16
    off[("eye2", 0)] = o
    o += 2
    return off, o


VOFF, NVEC = _vec_layout()


class Builder:
    def __init__(self, nstage=3 * DEPTH, final=True):
        self.nstage = nstage
        self.final = final
        self.S = Sched()
        self.psi = 0

    def next_ps(self):
        base = self.PS[self.psi % 8]
        v = View(base.ap, base.keys, self.psi)
        self.S.ps_cur[self.psi % 8] = self.psi
        self.psi += 1
        return v

    def mm(self, ps, lhsT, rhs, start, stop, ps_ap=None):
        o = ps.ap if ps_ap is None else ps_ap
        self.S.add("pe", lambda e: e.matmul(o, lhsT=lhsT.ap, rhs=rhs.ap, start=start, stop=stop),
                   reads=[lhsT, rhs], writes=[ps])

    def act(self, out, in_, func, bias=None, scale=None, extra_reads=()):
        kw = {}
        if bias is not None:
            kw["bias"] = bias
        if scale is not None:
            kw["scale"] = scale
        self.S.add("act", lambda e: e.activation(out=out.ap, in_=in_.ap, func=func, **kw),
                   reads=[in_] + list(extra_reads), writes=[out])

    def tt(self, eng, out, in0, in1, op):
        self.S.add(eng, lambda e: e.tensor_tensor(out=out.ap, in0=in0.ap, in1=in1.ap, op=op),
                   reads=[in0, in1], writes=[out])

    def ts(self, eng, out, in0, s1, s2, op0, op1=None, extra_reads=()):
        if op1 is None:
            fn = lambda e: e.tensor_scalar(out=out.ap, in0=in0.ap, scalar1=s1, scalar2=None, op0=op0)
        else:
            fn = lambda e: e.tensor_scalar(out=out.ap, in0=in0.ap, scalar1=s1, scalar2=s2, op0=op0, op1=op1)
        self.S.add(eng, fn, reads=[in0] + list(extra_reads), writes=[out])

    def stt(self, out, in0, scalar, in1, op0, op1, extra_reads=()):
        self.S.add("dve", lambda e: e.scalar_tensor_tensor(out=out.ap, in0=in0.ap, scalar=scalar,
                                                           in1=in1.ap, op0=op0, op1=op1),
                   reads=[in0, in1] + list(extra_reads), writes=[out])

    def dma(self, q, out_ap, in_ap, reads=(), writes=()):
        return self.S.add(q, lambda e: e.dma_start(out=out_ap, in_=in_ap), reads=reads, writes=writes, dma=True)

    def build(self):
        nc = bass.Bass("TRN2", target_bir_lowering=False)
        self.nc = nc
        dr = {}

        def din(name, shape, dt=F32):
            dr[name] = nc.dram_tensor(name, list(shape), dt, kind="ExternalInput").ap()

        din("xT", [NB, D, T])
        din("cT", [128, DC, NB])
        din("vec", [128, NVEC])
        din("cst", [128, 5, 128], BF16)
        din("cs", [NKT, 128, 2 * 16 * KTW], BF16)
        din("wmodt", [NMT, 128, DC * 512])
        din("wgu", [DEPTH, 2, NFC, 128, 2 * DC * 128])
        din("wd", [DEPTH, 2, NFC, DC, 128, 128])
        din("wconf", [DEPTH, 4, 128, 2 * DC * 128])
        din("wshort", [DEPTH, 4, 128, 3 * DC * 128])
        din("wfT", [DEPTH, 128, 4 * D])
        din("wmg", [DEPTH, 3, DC, 128, 12 * 128])
        din("wo", [DEPTH, DC, 128, DC * 128])
        outT = nc.dram_tensor("outT", [NB, D, T], F32, kind="ExternalOutput").ap()
        self.dr = dr

        oX = 0
        oH = oX + 65536
        oA = oH + 32768
        oB = oA + 32768
        oZ = oB + 32768
        oF = oZ + 16384
        oT = oF + 16384
        T_SQB = oT
        T_RS = T_SQB + 2048
        T_TMP = T_RS + 4096
        oC = T_TMP + 4096
        C_CST = oC
        C_VEC = C_CST + 1280
        C_CACT = C_VEC + 4 * NVEC
        C_MODB = C_CACT + 64
        n_modb = DEPTH * NB * 72 + NB * 16
        C_SCAL = C_MODB + 4 * n_modb
        n_scal = DEPTH * NB * 48 + NB * 8
        total = C_SCAL + 4 * n_scal
        total = (total + 63) // 64 * 64
        assert total <= 212992, total
        AR = Arena(total)
        self.AR = AR

        with contextlib.ExitStack() as st:
            AR.t = st.enter_context(nc.sbuf_tensor("arena", [128, total], U8))
            pst = [st.enter_context(nc.psum_tensor("ps%d" % i, [128, 512], F32)) for i in range(8)]
            self.PS = [View(pst[i][:, :], [("ps", i)]) for i in range(8)]

            X = [[AR.view(oX + (dc * T + t * TW) * 4, F32, TW) for t in range(TT)] for dc in range(DC)]
            Xrow = [AR.view(oX + dc * T * 4, F32, T) for dc in range(DC)]
            H = [[AR.view(oH + (dc * T + t * TW) * 2, BF16, TW) for t in range(TT)] for dc in range(DC)]
            self.X, self.H = X, H
            cst = AR.view(C_CST, BF16, 640, (5, 128))
            ident = cst.sub(cst.ap[:, 0, :])
            ones = cst.sub(cst.ap[:, 1, :])
            c64 = cst.sub(cst.ap[:, 2, :])
            s64 = cst.sub(cst.ap[:, 3, :])
            self.csn = cst.sub(cst.ap[:, 4, 0:16])
            self.ident, self.ones, self.c64, self.s64 = ident, ones, c64, s64
            vec = AR.view(C_VEC, F32, NVEC)
            self.vec = vec
            sqb = [AR.view(T_SQB + i * 1024, BF16, TW) for i in range(2)]
            rs = [AR.view(T_RS + i * 2048, F32, TW) for i in range(2)]
            tmp = [AR.view(T_TMP + i * 2048, F32, TW) for i in range(2)]
            self.sqb, self.rs, self.tmp = sqb, rs, tmp
            self.cnt = {"sqb": 0, "rs": 0, "tmp": 0}

            def vcol(name, l, j):
                o = VOFF[(name, l)] + j
                return vec.ap[:, o:o + 1]
            self.vcol = vcol

            def modb(l, b):
                return AR.view(C_MODB + 4 * ((l * NB + b) * 72), F32, 72)

            def modfin(b):
                return AR.view(C_MODB + 4 * (DEPTH * NB * 72 + b * 16), F32, 16)

            def scal(l, b):
                return AR.view(C_SCAL + 4 * ((l * NB + b) * 48), F32, 48)

            def scalfin(b):
                return AR.view(C_SCAL + 4 * (DEPTH * NB * 48 + b * 8), F32, 8)
            self.modb, self.modfin, self.scal, self.scalfin = modb, modfin, scal, scalfin

            self.dma("sp", cst.ap, dr["cst"], writes=[cst])
            self.dma("sp", vec.ap, dr["vec"], writes=[vec])
            craw = AR.view(oT, F32, 16, (DC, NB))
            self.dma("sp", craw.ap, dr["cT"], writes=[craw])
            cact16 = AR.view(C_CACT, BF16, 16, (DC, NB))
            self.act(cact16, craw, AF.Silu)
            self.cact16 = cact16
            self.load_x(0)
            self.o = dict(oA=oA, oB=oB, oZ=oZ, oF=oF)
            self.mod_pending = list(range(NMT))
            self.pre_wgu = {}
            self.mod_step()
            self.mod_step()
            if self.nstage > 0:
                for fc in range(3):
                    w = AR.view(oB + fc * 4096, BF16, 2048, (2, DC, 128))
                    self.dma("pool", w.ap.rearrange("p a b c -> p (a b c)"), dr["wgu"][0, 0, fc], writes=[w])
                    self.pre_wgu[(0, 0, 0, fc)] = w
                self._ngu = 3
            for _ in range(4):
                self.mod_step()

            out_ops = []
            self.out_ops = out_ops
            self.outT = outT
            self.deferred = []
            self.sqz = [AR.view(oZ + i * 1024, BF16, TW) for i in range(16)]
            self._r = {}
            full = (self.nstage == 3 * DEPTH and self.final)
            for b in range(NB):
                if b > 0 and not full:
                    self.load_x(b)
                phases = []
                for l in range(DEPTH):
                    phases += [("ffn", l, 0), ("mix", l, 1), ("ffn", l, 1)]
                phases = phases[:self.nstage]
                descs = [self.phase_desc(b, p) for p in phases]
                if full:
                    descs.append(self.final_desc(b))
                else:
                    descs.append(None)
                if descs[0] is not None:
                    self.norm_start(descs[0])
                for i, p in enumerate(phases):
                    if p[0] == "mix":
                        self.mixer(b, p[1], descs[i + 1])
                    else:
                        self.ffn(b, p[1], p[2], descs[i + 1])
                    if b == 0 and i == 0:
                        self.mod_finish()
                if b == 0 and not phases:
                    self.mod_finish()
                self.flush(all_=True)
                if not full:
                    out_ops += self.finish(b, outT)
            self.S.analyze()
            self.S.emit(nc, final_wait_ops=out_ops)
        return nc

    def mod_step(self):
        if not self.mod_pending:
            return
        i = self.mod_pending.pop(0)
        AR, vec = self.AR, self.vec
        if i < 18 * DEPTH:
            l, ct = i // 18, i % 18
            dst = [self.modb(l, b) for b in range(NB)]
            bo = VOFF[("bmod", l)]
        else:
            ct = i - 18 * DEPTH
            dst = [self.modfin(b) for b in range(NB)]
            bo = VOFF[("bfin", 0)]
        slot = AR.view(self.o["oF"] + (i % 2) * 8192, BF16, DC * 512, (DC, 512))
        self.dma("pool", slot.ap.rearrange("p a b -> p (a b)"), self.dr["wmodt"][i], writes=[slot])
        ps = self.next_ps()
        for dc in range(DC):
            self.mm(ps, self.cact16.sub(self.cact16.ap[:, dc, :]), slot.sub(slot.ap[:, dc, :]),
                    dc == 0, dc == DC - 1, ps_ap=ps.ap[0:NB, :])
        row = self.rot("rs")
        rowv = row.sub(row.ap[0:NB, :])
        self.S.add("dve", lambda e: e.tensor_copy(out=rowv.ap, in_=ps.ap[0:NB, :]), reads=[ps], writes=[row])
        pt = self.next_ps()
        eo = VOFF[("eye2", 0)]
        eye = vec.sub(vec.ap[0:NB, eo:eo + NB])
        for q in range(4):
            self.mm(pt, row.sub(row.ap[0:NB, q * 128:(q + 1) * 128]), eye, True, True,
                    ps_ap=pt.ap[:, NB * q:NB * q + NB])
        ptv = pt.ap[:, 0:4 * NB].rearrange("p (q b) -> p q b", b=NB)
        for b in range(NB):
            d_ = dst[b]
            self.tt("dve", d_.sub(d_.ap[:, ct * 4:ct * 4 + 4]), pt.sub(ptv[:, :, b]),
                    vec.sub(vec.ap[:, bo + ct * 4:bo + ct * 4 + 4]), ALU.add)
        if i < 18 * DEPTH and ct % 6 == 5:
            self.derive(i // 18, ct // 6)
        if i == NMT - 1:
            self.derive_final()

    def derive(self, l, n_only=None):
        vec = self.vec
        for b in range(NB):
            mb = self.modb(l, b)
            sc_ = self.scal(l, b)
            for n in range(3):
                if n_only is not None and n != n_only:
                    continue
                a_out = sc_.sub(sc_.ap[:, n * 8:(n + 1) * 8])
                scl = mb.sub(mb.ap[:, (3 * n + 1) * 8:(3 * n + 2) * 8])
                gn = vec.sub(vec.ap[:, VOFF[("g%d" % n, l)]:VOFF[("g%d" % n, l)] + 8])
                self.stt(a_out, scl, 1.0, gn, ALU.add, ALU.mult)
                g_out = sc_.sub(sc_.ap[:, 24 + n * 8:24 + (n + 1) * 8])
                gt = mb.sub(mb.ap[:, (3 * n + 2) * 8:(3 * n + 3) * 8])
                self.ts("dve", g_out, gt, 1.0 if n == 1 else 0.5, None, ALU.mult)

    def derive_final(self):
        vec = self.vec
        for b in range(NB):
            mb = self.modfin(b)
            sf = self.scalfin(b)
            gf = vec.sub(vec.ap[:, VOFF[("gf", 0)]:VOFF[("gf", 0)] + 8])
            self.stt(sf, mb.sub(mb.ap[:, 8:16]), 1.0, gf, ALU.add, ALU.mult)

    def mod_finish(self):
        while self.mod_pending:
            self.mod_step()

    def load_x_tile(self, b, t):
        for dc in range(DC):
            xr = self.X[dc][t]
            self.dma("sp", xr.ap, self.dr["xT"][b, dc * 128:(dc + 1) * 128, t * TW:(t + 1) * TW], writes=[xr])

    def load_x(self, b):
        for t in range(TT):
            self.load_x_tile(b, t)

    def rot(self, name):
        lst = getattr(self, name)
        i = self.cnt[name]
        self.cnt[name] = i + 1
        return lst[i % len(lst)]

    def rstd_tile(self, t):
        self.norm_squares(t)
        return self.norm_stats(t)

    def norm_squares(self, t):
        for dc in range(DC):
            self.act(self.sqz[(t * DC + dc) % 16], self.X[dc][t], AF.Square)

    def norm_stats(self, t):
        ps = self.next_ps()
        for dc in range(DC):
            self.mm(ps, self.ones, self.sqz[(t * DC + dc) % 16], dc == 0, dc == DC - 1)
        r = self.rot("rs")
        self.ts("dve", r, ps, 1.0 / D, EPS, ALU.mult, ALU.add)
        self.act(r, r, AF.Sqrt)
        self.S.add("dve", lambda e: e.reciprocal(out=r.ap, in_=r.ap), reads=[r], writes=[r])
        self._r[t] = r
        return r

    def norm_apply(self, nd, t):
        A_view, sh_view, out_fn, post = nd
        r = self._r[t]
        for dc in range(DC):
            tm = self.rot("tmp")
            self.stt(tm, self.X[dc][t], A_view.ap[:, dc:dc + 1], r, ALU.mult, ALU.mult,
                     extra_reads=[A_view])
            self.act(out_fn(dc, t), tm, AF.Identity, bias=sh_view.ap[:, dc:dc + 1],
                     extra_reads=[sh_view])
        if post is not None:
            post(t)

    def norm_full(self, nd):
        for t in range(TT):
            self.norm_squares(t)
            self.norm_stats(t)
            self.norm_apply(nd, t)

    def tail_mid(self, nd, t):
        if nd is None or t == 0:
            return
        self.norm_stats(t - 1)
        self.norm_apply(nd, t - 1)

    def tail_end(self, nd, t):
        if nd is None:
            return
        self.norm_squares(t)
        if t == TT - 1:
            def last():
                self.norm_stats(TT - 1)
                self.norm_apply(nd, TT - 1)
            self.deferred.append(last)

    def flush(self, all_=False):
        while self.deferred:
            self.deferred.pop(0)()
            if not all_:
                break

    def norm_start(self, nd):
        self.norm_squares(0)
        self.norm_squares(1)
        self.norm_stats(0)
        self.norm_stats(1)
        self.norm_apply(nd, 0)
        self.norm_apply(nd, 1)
        self.norm_squares(2)
        self.norm_squares(3)
        for t in (2, 3):
            def st(t=t):
                self.norm_stats(t)
                self.norm_apply(nd, t)
            self.deferred.append(st)

    def phase_desc(self, b, p):
        kind, l, f = p
        n = 1 if kind == "mix" else (0 if f == 0 else 2)
        mb = self.modb(l, b)
        sc_ = self.scal(l, b)
        A_view = sc_.sub(sc_.ap[:, n * 8:(n + 1) * 8])
        sh_view = mb.sub(mb.ap[:, (3 * n) * 8:(3 * n + 1) * 8])
        return (A_view, sh_view, lambda dc, t: self.H[dc][t], None)

    def final_desc(self, b):
        AR = self.AR
        oH = 65536
        sf = self.scalfin(b)
        mf = self.modfin(b)
        sh_view = mf.sub(mf.ap[:, 0:8])

        def out_fn(dc, t):
            return AR.view(oH + (t % 2) * 16384 + dc * TW * 4, F32, TW)

        def post(t):
            sg = AR.view(oH + (t % 2) * 16384, F32, DC * TW, (DC, TW))
            dst = self.outT[b].rearrange("(a p) t -> p a t", p=128)[:, :, t * TW:(t + 1) * TW]
            self.out_ops.append(self.dma("sp", dst, sg.ap, reads=[sg]))
            if b + 1 < NB:
                self.load_x_tile(b + 1, t)
        return (sf, sh_view, out_fn, post)

    def ffn(self, b, l, f, next_nd=None):
        AR, dr, o = self.AR, self.dr, self.o
        n = 0 if f == 0 else 2
        mb = self.modb(l, b)
        sc_ = self.scal(l, b)
        A_view = sc_.sub(sc_.ap[:, n * 8:(n + 1) * 8])
        sh_view = mb.sub(mb.ap[:, (3 * n) * 8:(3 * n + 1) * 8])
        hg = sc_.sub(sc_.ap[:, 24 + n * 8:24 + (n + 1) * 8])
        abuf = [[AR.view(o["oA"] + (j * T + t * TW) * 2, BF16, TW) for t in range(TT)] for j in range(8)]
        wgu = [AR.view(o["oB"] + i * 4096, BF16, 2048, (2, DC, 128)) for i in range(3)]
        wdb = [AR.view(o["oB"] + 12288 + i * 2048, BF16, 1024, (8, 128)) for i in range(2)]
        ngu = getattr(self, "_ngu", 0)
        nwd = getattr(self, "_nwd", 0)
        fc0 = 0
        GL = FSPLIT[-1]
        wall = [AR.view(o["oB"] + 16384 + dc * GL * 256, BF16, GL * 128, (GL, 128)) for dc in range(DC)]
        for gi, grp in enumerate(FSPLIT):
            lastg = (gi == len(FSPLIT) - 1)
            wts = {}

            def load_w(j):
                nonlocal ngu
                fc = fc0 + j
                key = (b, l, f, fc)
                if key in self.pre_wgu:
                    wts[j] = self.pre_wgu.pop(key)
                    return
                w = wgu[ngu % 3]
                ngu += 1
                self.dma("pool", w.ap.rearrange("p a b c -> p (a b c)"), dr["wgu"][l, f, fc], writes=[w])
                wts[j] = w
                if lastg and j == 0:
                    for dc in range(DC):
                        self.dma("pool", wall[dc].ap,
                                 dr["wd"][l, f, fc0:fc0 + grp, dc].rearrange("a p d -> p a d"), writes=[wall[dc]])

            def block(j, t):
                w = wts[j]
                pg = self.next_ps()
                pu = self.next_ps()
                for dc in range(DC):
                    self.mm(pg, w.sub(w.ap[:, 0, dc, :]), self.H[dc][t], dc == 0, dc == DC - 1)
                for dc in range(DC):
                    self.mm(pu, w.sub(w.ap[:, 1, dc, :]), self.H[dc][t], dc == 0, dc == DC - 1)
                sg = self.rot("tmp")
                self.act(sg, pg, AF.Silu)
                self.tt("dve", abuf[j][t], sg, pu, ALU.mult)

            def after_fc(j):
                fc = fc0 + j
                self.mod_step()
                if len(self.mod_pending) > NFC - 1 - fc:
                    self.mod_step()

            j_start = 0
            if gi == 0:
                load_w(0)
                load_w(1)
                for (j, t) in ((0, 0), (0, 1), (1, 0), (1, 1), (0, 2), (1, 2)):
                    block(j, t)
                    self.flush()
                self.flush(all_=True)
                for j in (0, 1):
                    block(j, TT - 1)
                    after_fc(j)
                j_start = 2
            for j in range(j_start, grp):
                load_w(j)
                for t in range(TT):
                    block(j, t)
                after_fc(j)
            if lastg:
                assert grp == GL
                for t in range(TT):
                    for dc in range(DC):
                        w = wall[dc]
                        py = self.next_ps()
                        for j in range(grp):
                            self.mm(py, w.sub(w.ap[:, j, :]), abuf[j][t], j == 0, j == grp - 1)
                        self.stt(self.X[dc][t], py, hg.ap[:, dc:dc + 1], self.X[dc][t], ALU.mult, ALU.add,
                                 extra_reads=[hg])
                        if dc == 4:
                            self.tail_mid(next_nd, t)
                    self.tail_end(next_nd, t)
                fc0 += grp
                continue
            for dc in range(DC):
                w = wdb[nwd % 2]
                nwd += 1
                wv = w.sub(w.ap[:, 0:grp, :])
                self.dma("pool", wv.ap,
                         dr["wd"][l, f, fc0:fc0 + grp, dc].rearrange("a p d -> p a d"), writes=[w])
                for t in range(TT):
                    py = self.next_ps()
                    for j in range(grp):
                        self.mm(py, w.sub(w.ap[:, j, :]), abuf[j][t], j == 0, j == grp - 1)
                    self.stt(self.X[dc][t], py, hg.ap[:, dc:dc + 1], self.X[dc][t], ALU.mult, ALU.add,
                             extra_reads=[hg])
            fc0 += grp
        self._ngu, self._nwd = ngu, nwd

    def mixer(self, b, l, next_nd=None):
        AR, dr, o, vcol = self.AR, self.dr, self.o, self.vcol
        oA, oB, oZ, oF = o["oA"], o["oB"], o["oZ"], o["oF"]
        mb = self.modb(l, b)
        sc_ = self.scal(l, b)
        g2 = sc_.sub(sc_.ap[:, 32:40])
        H, X = self.H, self.X

        VP = 2080
        vp = [AR.view(oA + cc * VP * 2, BF16, VP) for cc in range(4)]
        oD = oA + 4 * VP * 2
        oD = (oD + 255) // 256 * 256
        Dm = [AR.view(oD + i * 31 * 256, BF16, 31 * 128, (31, 128)) for i in range(2)]
        assert oD + 2 * 31 * 256 <= oA + 32768
        cv = [[AR.view(oB + (cc * T + t * TW) * 4, F32, TW) for t in range(TT)] for cc in range(4)]
        z = [[AR.view(oZ + (cc * T + t * TW) * 2, BF16, TW) for t in range(TT)] for cc in range(4)]
        wcf = [AR.view(oB + 24576 + i * 4096, BF16, 2048, (2, DC, 128)) for i in range(2)]
        assert 16384 + DC * FSPLIT[-1] * 256 <= 28672
        for cc in range(4):
            pad0 = vp[cc].sub(vp[cc].ap[:, 0:16])
            pad1 = vp[cc].sub(vp[cc].ap[:, VP - 16:VP])
            self.S.add("dve", lambda e, p=pad0: e.memset(p.ap, 0.0), writes=[pad0])
            self.S.add("dve", lambda e, p=pad1: e.memset(p.ap, 0.0), writes=[pad1])

        def build_D(cc):
            for k in range(31):
                dk = AR.view(oD + (cc % 2) * 31 * 256 + k * 256, BF16, 128)
                self.ts("dve", dk, self.ident, vcol("ccw", l, k * 4 + cc), None, ALU.mult,
                        extra_reads=[self.vec])
        wcf0 = AR.view(oB + 12288, BF16, 2048, (2, DC, 128))
        wconf = {}

        def load_conf(cc):
            w = wcf0 if cc == 0 else wcf[cc % 2]
            self.dma("pool", w.ap.rearrange("p a b c -> p (a b c)"), dr["wconf"][l, cc], writes=[w])
            wconf[cc] = w
        load_conf(0)
        load_conf(1)
        wft = AR.view(oZ, BF16, 4 * D, (4, D))
        self.dma("pool", wft.ap.rearrange("p a b -> p (a b)"), dr["wfT"][l], writes=[wft])
        c64, s64 = self.c64, self.s64

        def prep_w(Wo, cmat, d0, d1):
            for dcn in range(d0, d1):
                ps = self.next_ps()
                for cc in range(4):
                    lhs = wft.sub(wft.ap[:, cc, dcn * 128:(dcn + 1) * 128])
                    self.mm(ps, lhs, cmat, True, True, ps_ap=ps.ap[:, cc * 128:(cc + 1) * 128])
                wrow = AR.view(Wo + dcn * 1024, BF16, 512)
                self.S.add("dve", lambda e, o_=wrow, p_=ps: e.tensor_copy(out=o_.ap, in_=p_.ap),
                           reads=[ps], writes=[wrow])
        build_D(0)
        build_D(1)

        def proj(cc, t):
            w = wconf[cc]
            pa = self.next_ps()
            pb = self.next_ps()
            for dc in range(DC):
                self.mm(pa, w.sub(w.ap[:, 0, dc, :]), H[dc][t], dc == 0, dc == DC - 1)
            for dc in range(DC):
                self.mm(pb, w.sub(w.ap[:, 1, dc, :]), H[dc][t], dc == 0, dc == DC - 1)
            sg = self.rot("tmp")
            self.act(sg, pb, AF.Sigmoid)
            vt = AR.view(oA + (cc * VP + 16 + t * TW) * 2, BF16, TW)
            self.tt("dve", vt, sg, pa, ALU.mult)

        def conv(cc):
            for t in range(TT):
                pc = self.next_ps()
                for k in range(31):
                    dk = AR.view(oD + (cc % 2) * 31 * 256 + k * 256, BF16, 128)
                    rhs = AR.view(oA + (cc * VP + t * TW + k + 1) * 2, BF16, TW)
                    self.mm(pc, dk, rhs, k == 0, k == 30)
                self.act(cv[cc][t], pc, AF.Identity, bias=vcol("ccb", l, cc), extra_reads=[self.vec])

        for (cc, t) in ((0, 0), (0, 1), (1, 0), (1, 1), (0, 2), (1, 2)):
            proj(cc, t)
            self.flush()
        self.flush(all_=True)
        prep_w(oF, c64, 0, DC)
        proj(0, TT - 1)
        proj(1, TT - 1)
        conv(0)
        load_conf(2)
        conv(1)
        for t in range(TT):
            proj(2, t)
        build_D(2)
        load_conf(3)
        conv(2)
        for t in range(TT):
            proj(3, t)
        build_D(3)
        conv(3)
        lnr = [AR.view(oF + 8192 + i * 1024, BF16, TW) for i in range(8)]
        lnc = [0]

        def ln_tmp():
            v = lnr[lnc[0] % 8]
            lnc[0] += 1
            return v

        def ln_act(t):
            tm_ = []
            for cc in range(4):
                cb = ln_tmp()
                self.S.add("pool", lambda e, o_=cb, i_=cv[cc][t]: e.tensor_copy(out=o_.ap, in_=i_.ap),
                           reads=[cv[cc][t]], writes=[cb])
                tm_.append(cb)
            for cc in range(4):
                cq = ln_tmp()
                self.act(cq, cv[cc][t], AF.Square)
                tm_.append(cq)
            return tm_

        def ln_pe(tm_):
            p1 = self.next_ps()
            p2 = self.next_ps()
            for cc in range(4):
                self.mm(p1, self.ones, tm_[cc], cc == 0, cc == 3)
            for cc in range(4):
                self.mm(p2, self.ones, tm_[4 + cc], cc == 0, cc == 3)
            return p1, p2

        def ln_b(t, p1, p2):
            m = self.rot("rs")
            self.ts("dve", m, p1, 1.0 / 512, None, ALU.mult)
            var = self.rot("rs")
            self.tt("dve", var, m, m, ALU.mult)
            self.stt(var, p2, 1.0 / 512, var, ALU.mult, ALU.subtract)
            self.ts("dve", var, var, EPS, None, ALU.add)
            self.act(var, var, AF.Sqrt)
            self.S.add("dve", lambda e, r=var: e.reciprocal(out=r.ap, in_=r.ap), reads=[var], writes=[var])
            for cc in range(4):
                tm = self.rot("tmp")
                self.tt("dve", tm, cv[cc][t], m, ALU.subtract)
                self.tt("dve", tm, tm, var, ALU.mult)
                self.act(z[cc][t], tm, AF.Silu, bias=vcol("lnb", l, cc), scale=vcol("lng", l, cc),
                         extra_reads=[self.vec])

        uc = [AR.view(oA + sc * 1024, BF16, 512) for sc in range(16)]
        us = [AR.view(oA + 16384 + sc * 1024, BF16, 512) for sc in range(16)]

        def g3(v, c0, n, rev=False):
            a_ = v.ap
            ps_ = a_.ap[0]
            if rev:
                return AP(tensor=a_.tensor, offset=a_.offset + c0 + n - 1, ap=[[ps_[0], ps_[1]], [64, 8], [-1, n]])
            return AP(tensor=a_.tensor, offset=a_.offset + c0, ap=[[ps_[0], ps_[1]], [64, 8], [1, n]])

        usall = AR.view(oA + 16384, BF16, 16 * 512)
        for c0 in (0, 32):
            a_ = usall.ap
            zap = AP(tensor=a_.tensor, offset=a_.offset + c0, ap=[[a_.ap[0][0], a_.ap[0][1]], [64, 128], [1, 1]])
            self.S.add("dve", lambda e, z_=zap: e.memset(z_, 0.0), writes=[usall])

        def uproj(sc0, sc1):
            for sc in range(sc0, sc1):
                t, q = sc // 4, sc % 4
                ps = self.next_ps()
                for dc in range(DC):
                    lhs = H[dc][t].sub(H[dc][t].ap[:, q * 128:(q + 1) * 128])
                    rhs = AR.view(oF + dc * 1024, BF16, 512)
                    self.mm(ps, lhs, rhs, dc == 0, dc == DC - 1)
                self.S.add("act", lambda e, o_=g3(uc[sc], 0, 33), i_=g3(ps, 0, 33):
                           e.activation(out=o_, in_=i_, func=AF.Copy), reads=[ps], writes=[uc[sc]])
                self.S.add("dve", lambda e, o_=g3(uc[sc], 33, 31, rev=True), i_=g3(ps, 1, 31):
                           e.tensor_copy(out=o_, in_=i_), reads=[ps], writes=[uc[sc]])
                self.S.add("dve", lambda e, o_=g3(us[sc], 1, 31), i_=g3(ps, 33, 31):
                           e.tensor_copy(out=o_, in_=i_), reads=[ps], writes=[us[sc]])
                self.S.add("act", lambda e, o_=g3(us[sc], 33, 31, rev=True), i_=g3(ps, 33, 31):
                           e.activation(out=o_, in_=i_, func=AF.Copy, scale=-1.0), reads=[ps], writes=[us[sc]])

        fill = []
        for sc0 in range(0, 16, 2):
            fill.append(lambda sc0=sc0: uproj(sc0, sc0 + 2))
        ta = ln_act(0)
        fill.pop(0)()
        pa = ln_pe(ta)
        for t in range(TT):
            if t + 1 < TT:
                ta = ln_act(t + 1)
            fill.pop(0)()
            ln_b(t, *pa)
            if t + 1 < TT:
                pa = ln_pe(ta)
        while fill:
            fill.pop(0)()
        csb = [AR.view(oB + i * 16384, BF16, 2 * 16 * KTW, (2, 16, KTW)) for i in range(2)]
        for kt in range(NKT):
            cb = csb[kt % 2]
            self.dma("sp", cb.ap.rearrange("p a b c -> p (a b c)"), dr["cs"][kt], writes=[cb])
            for cc in range(4):
                pe_ = self.next_ps()
                po_ = self.next_ps()
                pev = pe_.ap[:, 0:KTW]
                pov = po_.ap[:, 0:KTW]
                for sc in range(16):
                    self.mm(pe_, uc[sc].sub(uc[sc].ap[:, cc * 128:(cc + 1) * 128]),
                            cb.sub(cb.ap[:, 0, sc, :]), sc == 0, sc == 15, ps_ap=pev)
                for sc in range(16):
                    self.mm(po_, us[sc].sub(us[sc].ap[:, cc * 128:(cc + 1) * 128]),
                            cb.sub(cb.ap[:, 1, sc, :]), sc == 0, sc == 15, ps_ap=pov)
                esb = self.rot("tmp")
                ev = esb.sub(esb.ap[:, 0:KTW])
                self.act(ev, pe_.sub(pev), AF.Copy)
                fo = AR.view(oF + (cc * T + kt * KTW) * 2, BF16, KTW)
                self.tt("dve", fo, ev, po_.sub(pov), ALU.add)
                j0 = 1 if kt == 0 else 0
                n = KTW - j0
                hi = T - kt * KTW - j0
                fh = AR.view(oF + (cc * T + hi - (n - 1)) * 2, BF16, n)
                a_ = fh.ap
                rap = AP(tensor=a_.tensor, offset=a_.offset + (n - 1), ap=[[a_.ap[0][0], a_.ap[0][1]], [-1, n]])
                self.tt("dve", View(rap, fh.keys), esb.sub(esb.ap[:, j0:KTW]), po_.sub(po_.ap[:, j0:KTW]),
                        ALU.subtract)
        for cc in range(4):
            pn = self.next_ps()
            for sc in range(16):
                self.mm(pn, uc[sc].sub(uc[sc].ap[:, cc * 128:(cc + 1) * 128]),
                        self.csn.sub(self.csn.ap[:, sc:sc + 1]), sc == 0, sc == 15, ps_ap=pn.ap[:, 0:1])
            fn_ = AR.view(oF + (cc * T + T // 2) * 2, BF16, 1)
            self.act(fn_, pn.sub(pn.ap[:, 0:1]), AF.Copy)

        mg = [[AR.view(oA + (dc * T + t * TW) * 2, BF16, TW) for t in range(TT)] for dc in range(DC)]
        wmgb = [AR.view(oB + 16640 + i * 3072, BF16, 12 * 128, (12, 128)) for i in range(3)]
        nmg = getattr(self, "_nmg", 0)

        def merge(i, src, first, after_dma2=None):
            nonlocal nmg
            for dc in range(DC):
                w = wmgb[nmg % 3]
                nmg += 1
                self.dma("pool", w.ap.rearrange("p a b -> p (a b)"), dr["wmg"][l, i, dc], writes=[w])
                if dc == 2 and after_dma2 is not None:
                    after_dma2()
                for t in range(TT):
                    pg = self.next_ps()
                    py = self.next_ps()
                    for di in range(DC):
                        self.mm(pg, w.sub(w.ap[:, di, :]), H[di][t], di == 0, di == DC - 1)
                    for cc in range(4):
                        self.mm(py, w.sub(w.ap[:, 8 + cc, :]), src(cc, t), cc == 0, cc == 3)
                    sg = self.rot("tmp")
                    self.act(sg, pg, AF.Sigmoid, bias=vcol("bgate", l, i * 8 + dc), extra_reads=[self.vec])
                    if first:
                        self.tt("dve", mg[dc][t], sg, py, ALU.mult)
                    else:
                        self.tt("dve", sg, sg, py, ALU.mult)
                        self.tt("dve", mg[dc][t], mg[dc][t], sg, ALU.add)

        merge(0, lambda cc, t: AR.view(oF + (cc * T + t * TW) * 2, BF16, TW), True)
        merge(2, lambda cc, t: z[cc][t], False)

        PP = 2052
        pp = [AR.view(oB + cc * PP * 2, BF16, PP) for cc in range(4)]
        assert 4 * PP * 2 <= 16640
        r = [[AR.view(oZ + (cc * T + t * TW) * 2, BF16, TW) for t in range(TT)] for cc in range(4)]
        d3 = [AR.view(oF + i * 256, BF16, 128) for i in range(12)]
        wsh = [AR.view(oF + 3072 + i * 6144, BF16, 3 * DC * 128, (3, DC, 128)) for i in range(2)]
        for cc in range(4):
            pad0 = pp[cc].sub(pp[cc].ap[:, 0:2])
            pad1 = pp[cc].sub(pp[cc].ap[:, PP - 2:PP])
            self.S.add("dve", lambda e, p=pad0: e.memset(p.ap, 0.0), writes=[pad0])
            self.S.add("dve", lambda e, p=pad1: e.memset(p.ap, 0.0), writes=[pad1])
            for k in range(3):
                self.ts("dve", d3[k * 4 + cc], self.ident, vcol("csw", l, k * 4 + cc), None, ALU.mult,
                        extra_reads=[self.vec])
        for cc in range(4):
            w = wsh[cc % 2]
            self.dma("pool", w.ap.rearrange("p a b c -> p (a b c)"), dr["wshort"][l, cc], writes=[w])
            for t in range(TT):
                p1 = self.next_ps()
                p2 = self.next_ps()
                for dc in range(DC):
                    self.mm(p1, w.sub(w.ap[:, 0, dc, :]), H[dc][t], dc == 0, dc == DC - 1)
                for dc in range(DC):
                    self.mm(p2, w.sub(w.ap[:, 1, dc, :]), H[dc][t], dc == 0, dc == DC - 1)
                tm = self.rot("tmp")
                self.act(tm, p1, AF.Copy)
                pt = AR.view(oB + (cc * PP + 2 + t * TW) * 2, BF16, TW)
                self.tt("dve", pt, tm, p2, ALU.mult)
            for t in range(TT):
                pq = self.next_ps()
                pbg = self.next_ps()
                for k in range(3):
                    rhs = AR.view(oB + (cc * PP + t * TW + k + 1) * 2, BF16, TW)
                    self.mm(pq, d3[k * 4 + cc], rhs, k == 0, k == 2)
                for dc in range(DC):
                    self.mm(pbg, w.sub(w.ap[:, 2, dc, :]), H[dc][t], dc == 0, dc == DC - 1)
                tm = self.rot("tmp")
                self.act(tm, pq, AF.Copy)
                self.tt("dve", r[cc][t], tm, pbg, ALU.mult)
        woall = [AR.view(oF + i * 2048, BF16, DC * 128, (DC, 128)) for i in range(DC)]

        def load_wo():
            for dcn in range(DC):
                self.dma("pool", woall[dcn].ap.rearrange("p a b -> p (a b)"), dr["wo"][l, dcn], writes=[woall[dcn]])
        merge(1, lambda cc, t: r[cc][t], False, after_dma2=load_wo)
        self._nmg = nmg

        for t in range(TT):
            for dcn in range(DC):
                w = woall[dcn]
                po = self.next_ps()
                for dc in range(DC):
                    self.mm(po, w.sub(w.ap[:, dc, :]), mg[dc][t], dc == 0, dc == DC - 1)
                self.stt(X[dcn][t], po, g2.ap[:, dcn:dcn + 1], X[dcn][t], ALU.mult, ALU.add,
                         extra_reads=[g2])
                if dcn == 4:
                    self.tail_mid(next_nd, t)
            self.tail_end(next_nd, t)

    def finish(self, b, outT):
        AR, o = self.AR, self.o
        ops = []
        oH = 65536
        stg = [AR.view(oH + i * 16384, F32, DC * TW, (DC, TW)) for i in range(2)]
        if self.final:
            sf = self.scalfin(b)
            mf = self.modfin(b)
        for t in range(TT):
            sg = stg[t % 2]
            if self.final:
                r = self.rstd_tile(t)
            for dc in range(DC):
                ot = AR.view(oH + (t % 2) * 16384 + dc * TW * 4, F32, TW)
                if self.final:
                    tm = self.rot("tmp")
                    self.stt(tm, self.X[dc][t], sf.ap[:, dc:dc + 1], r, ALU.mult, ALU.mult, extra_reads=[sf])
                    self.act(ot, tm, AF.Identity, bias=mf.ap[:, dc:dc + 1], extra_reads=[mf])
                else:
                    self.act(ot, self.X[dc][t], AF.Copy)
            dst = outT[b].rearrange("(a p) t -> p a t", p=128)[:, :, t * TW:(t + 1) * TW]
            ops.append(self.dma("sp", dst, sg.ap, reads=[sg]))
        return ops


def _consts():
    s = np.arange(T, dtype=np.float64)
    ang = 2.0 * np.pi * np.outer(s, s) / T
    C = np.cos(ang) / np.sqrt(T)
    Sn = -np.sin(ang) / np.sqrt(T)
    cs = np.stack([C, Sn], 0)[:, :, :NKT * KTW].reshape(2, 16, 128, NKT, KTW).transpose(3, 2, 0, 1, 4)
    cs = np.ascontiguousarray(cs).reshape(NKT, 128, 2 * 16 * KTW).astype(ml_dtypes.bfloat16)
    nyq = np.zeros((128, 128))
    nyq[:, 0:16] = C[:, T // 2].reshape(16, 128).T
    j = np.arange(64, dtype=np.float64)
    a64 = 2.0 * np.pi * np.outer(j, j) / 64
    c64 = np.zeros((128, 128))
    s64 = np.zeros((128, 128))
    for g in range(2):
        c64[g * 64:(g + 1) * 64, g * 64:(g + 1) * 64] = np.cos(a64) / 8.0
        s64[g * 64:(g + 1) * 64, g * 64:(g + 1) * 64] = np.sin(a64) / 8.0
    sel = np.zeros((128, 128))
    for g in range(2):
        sel[:, g * 64:g * 64 + 33] = c64[:, g * 64:g * 64 + 33]
        sel[:, g * 64 + 33:g * 64 + 64] = s64[:, g * 64 + 1:g * 64 + 32]
    cst = np.stack([np.eye(128), np.ones((128, 128)), sel, s64, nyq], 1).astype(ml_dtypes.bfloat16)
    return cs, np.ascontiguousarray(cst)


def _col(v):
    return np.ascontiguousarray(np.asarray(v, np.float32).reshape(-1, 128).T)


def _prep_shared(inp):
    f = lambda k: np.asarray(inp[k], np.float32)
    sh = {}
    vec = np.zeros((128, NVEC), np.float32)
    for l in range(DEPTH):
        def put(name, arr):
            o = VOFF[(name, l)]
            vec[:, o:o + arr.shape[1]] = arr
        put("g0", _col(f("ffn1_norm_g")[l]))
        put("g1", _col(f("mix_norm_g")[l]))
        put("g2", _col(f("ffn2_norm_g")[l]))
        put("bmod", _col(f("b_mod")[l]))
        csw = f("conv_short_w")[l].reshape(3, 4, 128).transpose(2, 0, 1).reshape(128, 12)
        put("csw", csw)
        ccw = f("conv_conf_w")[l].reshape(31, 4, 128).transpose(2, 0, 1).reshape(128, 124)
        put("ccw", ccw)
        put("ccb", _col(f("conv_conf_b")[l]))
        put("lng", _col(f("conf_ln_g")[l]))
        put("lnb", _col(f("conf_ln_b")[l]))
        put("bgate", _col(f("b_gate")[l]))
    o = VOFF[("gf", 0)]
    vec[:, o:o + 8] = _col(f("final_norm_g"))
    o = VOFF[("bfin", 0)]
    vec[:, o:o + 16] = _col(f("b_final_mod"))
    o = VOFF[("eye2", 0)]
    vec[0, o] = 1.0
    vec[1, o + 1] = 1.0
    sh["vec"] = vec
    cs, cst = _consts()
    sh["cs"] = cs
    sh["cst"] = cst
    wm = f("w_mod").reshape(DEPTH, DC, 128, 18, 512).transpose(0, 3, 2, 1, 4).reshape(DEPTH * 18, 128, DC * 512)
    wf = f("w_final_mod").reshape(DC, 128, 4, 512).transpose(2, 1, 0, 3).reshape(4, 128, DC * 512)
    sh["wmodt"] = np.ascontiguousarray(np.concatenate([wm, wf], 0))
    def gu(k):
        return f(k).reshape(DEPTH, DC, 128, NFC, 128).transpose(0, 3, 2, 1, 4)
    wgu = np.stack([np.stack([gu("ffn1_w_gate"), gu("ffn1_w_up")], 3),
                    np.stack([gu("ffn2_w_gate"), gu("ffn2_w_up")], 3)], 1)
    sh["wgu"] = np.ascontiguousarray(wgu).reshape(DEPTH, 2, NFC, 128, 2 * DC * 128)
    def dn(k):
        return f(k).reshape(DEPTH, NFC, 128, DC, 128).transpose(0, 1, 3, 2, 4)
    sh["wd"] = np.ascontiguousarray(np.stack([dn("ffn1_w_down"), dn("ffn2_w_down")], 1))
    w_in = f("w_in")
    def colgrp(c0):
        return w_in[:, :, c0:c0 + 512].reshape(DEPTH, DC, 128, 4, 128).transpose(0, 3, 2, 1, 4)
    u_bg, u_cg, u_x, u_ga, u_gb = (colgrp(512), colgrp(1024), colgrp(1536), colgrp(2048), colgrp(2560))
    sh["wconf"] = np.ascontiguousarray(np.stack([u_ga, u_gb], 3)).reshape(DEPTH, 4, 128, 2 * DC * 128)
    sh["wshort"] = np.ascontiguousarray(np.stack([u_cg, u_x, u_bg], 3)).reshape(DEPTH, 4, 128, 3 * DC * 128)
    wfT = w_in[:, :, 0:512].transpose(0, 2, 1).reshape(DEPTH, 4, 128, D).transpose(0, 2, 1, 3)
    sh["wfT"] = np.ascontiguousarray(wfT).reshape(DEPTH, 128, 4 * D)
    wg = f("w_gate").reshape(DEPTH, DC, 128, 3, DC, 128).transpose(0, 3, 4, 2, 1, 5)
    br = np.stack([f("w_branch_f"), f("w_branch_s"), f("w_branch_c")], 1)
    br = br.reshape(DEPTH, 3, 4, 128, DC, 128).transpose(0, 1, 4, 3, 2, 5)
    sh["wmg"] = np.ascontiguousarray(np.concatenate([wg, br], 4)).reshape(DEPTH, 3, DC, 128, 12 * 128)
    wo = f("w_out").reshape(DEPTH, DC, 128, DC, 128).transpose(0, 3, 2, 1, 4)
    sh["wo"] = np.ascontiguousarray(wo).reshape(DEPTH, DC, 128, DC * 128)
    return sh


_NC_CACHE = {}


def _get_nc(nstage=3 * DEPTH, final=True):
    key = (nstage, final)
    if key not in _NC_CACHE:
        _NC_CACHE[key] = Builder(nstage, final).build()
    return _NC_CACHE[key]


def kernel(_nstage=3 * DEPTH, _final=True, _ncores=8, **inp):
    x = np.asarray(inp["x"], np.float32)
    c = np.asarray(inp["c"], np.float32)
    sh = _prep_shared(inp)
    nc = _get_nc(_nstage, _final)
    in_maps = []
    for i in range(_ncores):
        xb = x[i * NB:(i + 1) * NB]
        m = dict(sh)
        m["xT"] = np.ascontiguousarray(xb.transpose(0, 2, 1))
        cb = c[i * NB:(i + 1) * NB]
        m["cT"] = np.ascontiguousarray(cb.reshape(NB, DC, 128).transpose(2, 1, 0))
        in_maps.append(m)
    res = run_bass_kernel_spmd(nc, in_maps, core_ids=list(range(_ncores)))
    outs = [np.asarray(r["outT"]).transpose(0, 2, 1) for r in res.results]
    return np.ascontiguousarray(np.concatenate(outs, 0)).astype(np.float32)
```

```python
import contextlib
import numpy as np
import ml_dtypes
import concourse.bass as bass
import concourse.mybir as mybir
from concourse.bass_utils import run_bass_kernel_spmd
from concourse.ap import AP

F32 = mybir.dt.float32
BF16 = mybir.dt.bfloat16
U8 = mybir.dt.uint8
AF = mybir.ActivationFunctionType
ALU = mybir.AluOpType

D = 1024
T = 2048
NB = 2
DEPTH = 2
DFF = 2816
NFC = 22
DC = 8
TT = 4
TW = 512
EPS = 1e-6
FSPLIT = (8, 8, 6)
NMT = 18 * DEPTH + 4
NKT = 4
KTW = 256

ENGS = ("pe", "act", "dve", "pool", "sp")
BLK = 256


class View:
    __slots__ = ("ap", "keys", "gen")

    def __init__(self, ap, keys, gen=None):
        self.ap = ap
        self.keys = keys
        self.gen = gen

    def sub(self, ap):
        return View(ap, self.keys, self.gen)


class Op:
    __slots__ = ("eng", "fn", "reads", "writes", "is_dma", "eidx", "waits",
                 "signal", "sem", "val", "prewait")

    def __init__(self, eng, fn, reads, writes, is_dma):
        self.eng = eng
        self.fn = fn
        self.reads = reads
        self.writes = writes
        self.is_dma = is_dma
        self.waits = []
        self.signal = False
        self.sem = None
        self.val = 0
        self.prewait = None


class Sched:
    def __init__(self, dma_ring=6):
        self.ops = []
        self.eng_ops = {e: [] for e in ENGS}
        self.K = dma_ring
        self.ps_cur = {}

    def add(self, eng, fn, reads=(), writes=(), dma=False):
        rk = []
        for v in reads:
            rk.extend(v.keys)
            if v.gen is not None:
                assert self.ps_cur[v.gen % 8] == v.gen, "PSUM bank recycled before its reader"
        wk = []
        for v in writes:
            wk.extend(v.keys)
        op = Op(eng, fn, rk, wk, dma)
        op.eidx = len(self.eng_ops[eng])
        self.eng_ops[eng].append(op)
        self.ops.append(op)
        return op

    def analyze(self):
        last_writer = {}
        readers = {}
        known = {e: {} for e in ENGS}
        known_dma = {e: set() for e in ENGS}
        for op in self.ops:
            deps = {}
            for k in op.reads:
                lw = last_writer.get(k)
                if lw is not None and lw is not op:
                    deps[id(lw)] = (lw, True)
            for k in op.writes:
                lw = last_writer.get(k)
                if lw is not None and lw is not op and id(lw) not in deps:
                    deps[id(lw)] = (lw, False)
                rl = readers.get(k)
                if rl:
                    for rd in rl:
                        if rd is not op and id(rd) not in deps:
                            deps[id(rd)] = (rd, False)
            best = {}
            for d, is_raw in deps.values():
                if (not d.is_dma) and (not op.is_dma) and d.eng == op.eng:
                    if d.eng == "pe":
                        continue
                if d.is_dma:
                    if id(d) in known_dma[op.eng]:
                        continue
                    known_dma[op.eng].add(id(d))
                    d.signal = True
                    op.waits.append(d)
                else:
                    cur = best.get(d.eng)
                    if cur is None or d.eidx > cur.eidx:
                        best[d.eng] = d
            for d in best.values():
                if known[op.eng].get(d.eng, -1) >= d.eidx:
                    continue
                known[op.eng][d.eng] = d.eidx
                d.signal = True
                op.waits.append(d)
            for k in op.reads:
                rl = readers.get(k)
                if rl is None:
                    readers[k] = [op]
                elif not rl or rl[-1] is not op:
                    rl.append(op)
            for k in op.writes:
                last_writer[k] = op
                readers[k] = []

    def emit(self, nc, final_wait_ops=()):
        with contextlib.ExitStack() as st:
            esem = {e: st.enter_context(nc.semaphore("s_" + e)) for e in ENGS}
            dsem = {}
            for e in ENGS:
                if any(o.is_dma for o in self.eng_ops[e]):
                    dsem[e] = [st.enter_context(nc.semaphore("d_%s%d" % (e, i)))
                               for i in range(self.K)]
            for e in ENGS:
                cnt = 0
                j = 0
                for op in self.eng_ops[e]:
                    if op.is_dma:
                        op.sem = dsem[e][j % self.K]
                        op.val = 16 * (j // self.K + 1)
                        if j >= self.K:
                            op.prewait = (op.sem, 16 * (j // self.K))
                        j += 1
                    elif op.signal:
                        cnt += 1
                        op.sem = esem[e]
                        op.val = cnt
            block = st.enter_context(nc.Block())

            def make(e):
                def body(eng):
                    for op in self.eng_ops[e]:
                        if op.prewait is not None:
                            eng.wait_ge(op.prewait[0], op.prewait[1])
                        for d in op.waits:
                            eng.wait_ge(d.sem, d.val)
                        if op.fn is None:
                            continue
                        ins = op.fn(eng)
                        if op.is_dma:
                            ins.then_inc(op.sem, 16)
                        elif op.signal:
                            ins.then_inc(op.sem, 1)
                    if e == "sp":
                        for o in final_wait_ops:
                            eng.wait_ge(o.sem, o.val)
                return body

            block.tensor(make("pe"))
            block.scalar(make("act"))
            block.vector(make("dve"))
            block.gpsimd(make("pool"))
            block.sync(make("sp"))


class Arena:
    def __init__(self, nbytes):
        self.nbytes = nbytes
        self.t = None

    def view(self, off, dtype, ncols, shape=None):
        esz = 4 if dtype == F32 else 2
        nb = ncols * esz
        assert off % esz == 0 and off >= 0 and off + nb <= self.nbytes, (off, nb, self.nbytes)
        ap = self.t[:, off:off + nb].bitcast(dtype)
        if shape is not None:
            if len(shape) == 2:
                ap = ap.rearrange("p (a b) -> p a b", a=shape[0], b=shape[1])
            elif len(shape) == 3:
                ap = ap.rearrange("p (a b c) -> p a b c", a=shape[0], b=shape[1], c=shape[2])
        return View(ap, range(off // BLK, (off + nb - 1) // BLK + 1))


def _vec_layout():
    off = {}
    o = 0
    for l in range(DEPTH):
        for name, n in (("g0", 8), ("g1", 8), ("g2", 8), ("bmod", 72), ("csw", 12),
                        ("ccw", 124), ("ccb", 4), ("lng", 4), ("lnb", 4), ("bgate", 24)):
            off[(name, l)] = o
            o += n
    off[("gf", 0)] = o
    o += 8
    off[("bfin", 0)] = o
    o += 16
    off[("eye2", 0)] = o
    o += 2
    return off, o


VOFF, NVEC = _vec_layout()


class Builder:
    def __init__(self, nstage=3 * DEPTH, final=True):
        self.nstage = nstage
        self.final = final
        self.S = Sched()
        self.psi = 0

    def next_ps(self):
        base = self.PS[self.psi % 8]
        v = View(base.ap, base.keys, self.psi)
        self.S.ps_cur[self.psi % 8] = self.psi
        self.psi += 1
        return v

    def mm(self, ps, lhsT, rhs, start, stop, ps_ap=None):
        o = ps.ap if ps_ap is None else ps_ap
        self.S.add("pe", lambda e: e.matmul(o, lhsT=lhsT.ap, rhs=rhs.ap, start=start, stop=stop),
                   reads=[lhsT, rhs], writes=[ps])

    def act(self, out, in_, func, bias=None, scale=None, extra_reads=()):
        kw = {}
        if bias is not None:
            kw["bias"] = bias
        if scale is not None:
            kw["scale"] = scale
        self.S.add("act", lambda e: e.activation(out=out.ap, in_=in_.ap, func=func, **kw),
                   reads=[in_] + list(extra_reads), writes=[out])

    def tt(self, eng, out, in0, in1, op):
        self.S.add(eng, lambda e: e.tensor_tensor(out=out.ap, in0=in0.ap, in1=in1.ap, op=op),
                   reads=[in0, in1], writes=[out])

    def ts(self, eng, out, in0, s1, s2, op0, op1=None, extra_reads=()):
        if op1 is None:
            fn = lambda e: e.tensor_scalar(out=out.ap, in0=in0.ap, scalar1=s1, scalar2=None, op0=op0)
        else:
            fn = lambda e: e.tensor_scalar(out=out.ap, in0=in0.ap, scalar1=s1, scalar2=s2, op0=op0, op1=op1)
        self.S.add(eng, fn, reads=[in0] + list(extra_reads), writes=[out])

    def stt(self, out, in0, scalar, in1, op0, op1, extra_reads=()):
        self.S.add("dve", lambda e: e.scalar_tensor_tensor(out=out.ap, in0=in0.ap, scalar=scalar,
                                                           in1=in1.ap, op0=op0, op1=op1),
                   reads=[in0, in1] + list(extra_reads), writes=[out])

    def dma(self, q, out_ap, in_ap, reads=(), writes=()):
        return self.S.add(q, lambda e: e.dma_start(out=out_ap, in_=in_ap), reads=reads, writes=writes, dma=True)

    def build(self):
        nc = bass.Bass("TRN2", target_bir_lowering=False)
        self.nc = nc
        dr = {}

        def din(name, shape, dt=F32):
            dr[name] = nc.dram_tensor(name, list(shape), dt, kind="ExternalInput").ap()

        din("xT", [NB, D, T])
        din("cT", [128, DC, NB])
        din("vec", [128, NVEC])
        din("cst", [128, 5, 128], BF16)
        din("cs", [NKT, 128, 2 * 16 * KTW], BF16)
        din("wmodt", [NMT, 128, DC * 512])
        din("wgu", [DEPTH, 2, NFC, 128, 2 * DC * 128])
        din("wd", [DEPTH, 2, NFC, DC, 128, 128])
        din("wconf", [DEPTH, 4, 128, 2 * DC * 128])
        din("wshort", [DEPTH, 4, 128, 3 * DC * 128])
        din("wfT", [DEPTH, 128, 4 * D])
        din("wmg", [DEPTH, 3, DC, 128, 12 * 128])
        din("wo", [DEPTH, DC, 128, DC * 128])
        outT = nc.dram_tensor("outT", [NB, D, T], F32, kind="ExternalOutput").ap()
        self.dr = dr

        oX = 0
        oH = oX + 65536
        oA = oH + 32768
        oB = oA + 32768
        oZ = oB + 32768
        oF = oZ + 16384
        oT = oF + 16384
        T_SQB = oT
        T_RS = T_SQB + 2048
        T_TMP = T_RS + 4096
        oC = T_TMP + 4096
        C_CST = oC
        C_VEC = C_CST + 1280
        C_CACT = C_VEC + 4 * NVEC
        C_MODB = C_CACT + 64
        n_modb = DEPTH * NB * 72 + NB * 16
        C_SCAL = C_MODB + 4 * n_modb
        n_scal = DEPTH * NB * 48 + NB * 8
        total = C_SCAL + 4 * n_scal
        total = (total + 63) // 64 * 64
        assert total <= 212992, total
        AR = Arena(total)
        self.AR = AR

        with contextlib.ExitStack() as st:
            AR.t = st.enter_context(nc.sbuf_tensor("arena", [128, total], U8))
            pst = [st.enter_context(nc.psum_tensor("ps%d" % i, [128, 512], F32)) for i in range(8)]
            self.PS = [View(pst[i][:, :], [("ps", i)]) for i in range(8)]

            X = [[AR.view(oX + (dc * T + t * TW) * 4, F32, TW) for t in range(TT)] for dc in range(DC)]
            Xrow = [AR.view(oX + dc * T * 4, F32, T) for dc in range(DC)]
            H = [[AR.view(oH + (dc * T + t * TW) * 2, BF16, TW) for t in range(TT)] for dc in range(DC)]
            self.X, self.H = X, H
            cst = AR.view(C_CST, BF16, 640, (5, 128))
            ident = cst.sub(cst.ap[:, 0, :])
            ones = cst.sub(cst.ap[:, 1, :])
            c64 = cst.sub(cst.ap[:, 2, :])
            s64 = cst.sub(cst.ap[:, 3, :])
            self.csn = cst.sub(cst.ap[:, 4, 0:16])
            self.ident, self.ones, self.c64, self.s64 = ident, ones, c64, s64
            vec = AR.view(C_VEC, F32, NVEC)
            self.vec = vec
            sqb = [AR.view(T_SQB + i * 1024, BF16, TW) for i in range(2)]
            rs = [AR.view(T_RS + i * 2048, F32, TW) for i in range(2)]
            tmp = [AR.view(T_TMP + i * 2048, F32, TW) for i in range(2)]
            self.sqb, self.rs, self.tmp = sqb, rs, tmp
            self.cnt = {"sqb": 0, "rs": 0, "tmp": 0}

            def vcol(name, l, j):
                o = VOFF[(name, l)] + j
                return vec.ap[:, o:o + 1]
            self.vcol = vcol

            def modb(l, b):
                return AR.view(C_MODB + 4 * ((l * NB + b) * 72), F32, 72)

            def modfin(b):
                return AR.view(C_MODB + 4 * (DEPTH * NB * 72 + b * 16), F32, 16)

            def scal(l, b):
                return AR.view(C_SCAL + 4 * ((l * NB + b) * 48), F32, 48)

            def scalfin(b):
                return AR.view(C_SCAL + 4 * (DEPTH * NB * 48 + b * 8), F32, 8)
            self.modb, self.modfin, self.scal, self.scalfin = modb, modfin, scal, scalfin

            self.dma("sp", cst.ap, dr["cst"], writes=[cst])
            self.dma("sp", vec.ap, dr["vec"], writes=[vec])
            craw = AR.view(oT, F32, 16, (DC, NB))
            self.dma("sp", craw.ap, dr["cT"], writes=[craw])
            cact16 = AR.view(C_CACT, BF16, 16, (DC, NB))
            self.act(cact16, craw, AF.Silu)
            self.cact16 = cact16
            self.load_x(0)
            self.o = dict(oA=oA, oB=oB, oZ=oZ, oF=oF)
            self.mod_pending = list(range(NMT))
            self.pre_wgu = {}
            self.mod_step()
            self.mod_step()
            if self.nstage > 0:
                for fc in range(3):
                    w = AR.view(oB + fc * 4096, BF16, 2048, (2, DC, 128))
                    self.dma("pool", w.ap.rearrange("p a b c -> p (a b c)"), dr["wgu"][0, 0, fc], writes=[w])
                    self.pre_wgu[(0, 0, 0, fc)] = w
                self._ngu = 3
            for _ in range(4):
                self.mod_step()

            out_ops = []
            self.out_ops = out_ops
            self.outT = outT
            self.deferred = []
            self.sqz = [AR.view(oZ + i * 1024, BF16, TW) for i in range(16)]
            self._r = {}
            full = (self.nstage == 3 * DEPTH and self.final)
            for b in range(NB):
                if b > 0 and not full:
                    self.load_x(b)
                phases = []
                for l in range(DEPTH):
                    phases += [("ffn", l, 0), ("mix", l, 1), ("ffn", l, 1)]
                phases = phases[:self.nstage]
                descs = [self.phase_desc(b, p) for p in phases]
                if full:
                    descs.append(self.final_desc(b))
                else:
                    descs.append(None)
                if descs[0] is not None:
                    self.norm_start(descs[0])
                for i, p in enumerate(phases):
                    if p[0] == "mix":
                        self.mixer(b, p[1], descs[i + 1])
                    else:
                        self.ffn(b, p[1], p[2], descs[i + 1])
                    if b == 0 and i == 0:
                        self.mod_finish()
                if b == 0 and not phases:
                    self.mod_finish()
                self.flush(all_=True)
                if not full:
                    out_ops += self.finish(b, outT)
            self.S.analyze()
            self.S.emit(nc, final_wait_ops=out_ops)
        return nc

    def mod_step(self):
        if not self.mod_pending:
            return
        i = self.mod_pending.pop(0)
        AR, vec = self.AR, self.vec
        if i < 18 * DEPTH:
            l, ct = i // 18, i % 18
            dst = [self.modb(l, b) for b in range(NB)]
            bo = VOFF[("bmod", l)]
        else:
            ct = i - 18 * DEPTH
            dst = [self.modfin(b) for b in range(NB)]
            bo = VOFF[("bfin", 0)]
        slot = AR.view(self.o["oF"] + (i % 2) * 8192, BF16, DC * 512, (DC, 512))
        self.dma("pool", slot.ap.rearrange("p a b -> p (a b)"), self.dr["wmodt"][i], writes=[slot])
        ps = self.next_ps()
        for dc in range(DC):
            self.mm(ps, self.cact16.sub(self.cact16.ap[:, dc, :]), slot.sub(slot.ap[:, dc, :]),
                    dc == 0, dc == DC - 1, ps_ap=ps.ap[0:NB, :])
        row = self.rot("rs")
        rowv = row.sub(row.ap[0:NB, :])
        self.S.add("dve", lambda e: e.tensor_copy(out=rowv.ap, in_=ps.ap[0:NB, :]), reads=[ps], writes=[row])
        pt = self.next_ps()
        eo = VOFF[("eye2", 0)]
        eye = vec.sub(vec.ap[0:NB, eo:eo + NB])
        for q in range(4):
            self.mm(pt, row.sub(row.ap[0:NB, q * 128:(q + 1) * 128]), eye, True, True,
                    ps_ap=pt.ap[:, NB * q:NB * q + NB])
        ptv = pt.ap[:, 0:4 * NB].rearrange("p (q b) -> p q b", b=NB)
        for b in range(NB):
            d_ = dst[b]
            self.tt("dve", d_.sub(d_.ap[:, ct * 4:ct * 4 + 4]), pt.sub(ptv[:, :, b]),
                    vec.sub(vec.ap[:, bo + ct * 4:bo + ct * 4 + 4]), ALU.add)
        if i < 18 * DEPTH and ct % 6 == 5:
            self.derive(i // 18, ct // 6)
        if i == NMT - 1:
            self.derive_final()

    def derive(self, l, n_only=None):
        vec = self.vec
        for b in range(NB):
            mb = self.modb(l, b)
            sc_ = self.scal(l, b)
            for n in range(3):
                if n_only is not None and n != n_only:
                    continue
                a_out = sc_.sub(sc_.ap[:, n * 8:(n + 1) * 8])
                scl = mb.sub(mb.ap[:, (3 * n + 1) * 8:(3 * n + 2) * 8])
                gn = vec.sub(vec.ap[:, VOFF[("g%d" % n, l)]:VOFF[("g%d" % n, l)] + 8])
                self.stt(a_out, scl, 1.0, gn, ALU.add, ALU.mult)
                g_out = sc_.sub(sc_.ap[:, 24 + n * 8:24 + (n + 1) * 8])
                gt = mb.sub(mb.ap[:, (3 * n + 2) * 8:(3 * n + 3) * 8])
                self.ts("dve", g_out, gt, 1.0 if n == 1 else 0.5, None, ALU.mult)

    def derive_final(self):
        vec = self.vec
        for b in range(NB):
            mb = self.modfin(b)
            sf = self.scalfin(b)
            gf = vec.sub(vec.ap[:, VOFF[("gf", 0)]:VOFF[("gf", 0)] + 8])
            self.stt(sf, mb.sub(mb.ap[:, 8:16]), 1.0, gf, ALU.add, ALU.mult)

    def mod_finish(self):
        while self.mod_pending:
            self.mod_step()

    def load_x_tile(self, b, t):
        for dc in range(DC):
            xr = self.X[dc][t]
            self.dma("sp", xr.ap, self.dr["xT"][b, dc * 128:(dc + 1) * 128, t * TW:(t + 1) * TW], writes=[xr])

    def load_x(self, b):
        for t in range(TT):
            self.load_x_tile(b, t)

    def rot(self, name):
        lst = getattr(self, name)
        i = self.cnt[name]
        self.cnt[name] = i + 1
        return lst[i % len(lst)]

    def rstd_tile(self, t):
        self.norm_squares(t)
        return self.norm_stats(t)

    def norm_squares(self, t):
        for dc in range(DC):
            self.act(self.sqz[(t * DC + dc) % 16], self.X[dc][t], AF.Square)

    def norm_stats(self, t):
        ps = self.next_ps()
        for dc in range(DC):
            self.mm(ps, self.ones, self.sqz[(t * DC + dc) % 16], dc == 0, dc == DC - 1)
        r = self.rot("rs")
        self.ts("dve", r, ps, 1.0 / D, EPS, ALU.mult, ALU.add)
        self.act(r, r, AF.Sqrt)
        self.S.add("dve", lambda e: e.reciprocal(out=r.ap, in_=r.ap), reads=[r], writes=[r])
        self._r[t] = r
        return r

    def norm_apply(self, nd, t):
        A_view, sh_view, out_fn, post = nd
        r = self._r[t]
        for dc in range(DC):
            tm = self.rot("tmp")
            self.stt(tm, self.X[dc][t], A_view.ap[:, dc:dc + 1], r, ALU.mult, ALU.mult,
                     extra_reads=[A_view])
            self.act(out_fn(dc, t), tm, AF.Identity, bias=sh_view.ap[:, dc:dc + 1],
                     extra_reads=[sh_view])
        if post is not None:
            post(t)

    def norm_full(self, nd):
        for t in range(TT):
            self.norm_squares(t)
            self.norm_stats(t)
            self.norm_apply(nd, t)

    def tail_mid(self, nd, t):
        if nd is None or t == 0:
            return
        self.norm_stats(t - 1)
        self.norm_apply(nd, t - 1)

    def tail_end(self, nd, t):
        if nd is None:
            return
        self.norm_squares(t)
        if t == TT - 1:
            def last():
                self.norm_stats(TT - 1)
                self.norm_apply(nd, TT - 1)
            self.deferred.append(last)

    def flush(self, all_=False):
        while self.deferred:
            self.deferred.pop(0)()
            if not all_:
                break

    def norm_start(self, nd):
        self.norm_squares(0)
        self.norm_squares(1)
        self.norm_stats(0)
        self.norm_stats(1)
        self.norm_apply(nd, 0)
        self.norm_apply(nd, 1)
        self.norm_squares(2)
        self.norm_squares(3)
        for t in (2, 3):
            def st(t=t):
                self.norm_stats(t)
                self.norm_apply(nd, t)
            self.deferred.append(st)

    def phase_desc(self, b, p):
        kind, l, f = p
        n = 1 if kind == "mix" else (0 if f == 0 else 2)
        mb = self.modb(l, b)
        sc_ = self.scal(l, b)
        A_view = sc_.sub(sc_.ap[:, n * 8:(n + 1) * 8])
        sh_view = mb.sub(mb.ap[:, (3 * n) * 8:(3 * n + 1) * 8])
        return (A_view, sh_view, lambda dc, t: self.H[dc][t], None)

    def final_desc(self, b):
        AR = self.AR
        oH = 65536
        sf = self.scalfin(b)
        mf = self.modfin(b)
        sh_view = mf.sub(mf.ap[:, 0:8])

        def out_fn(dc, t):
            return AR.view(oH + (t % 2) * 16384 + dc * TW * 4, F32, TW)

        def post(t):
            sg = AR.view(oH + (t % 2) * 16384, F32, DC * TW, (DC, TW))
            dst = self.outT[b].rearrange("(a p) t -> p a t", p=128)[:, :, t * TW:(t + 1) * TW]
            self.out_ops.append(self.dma("sp", dst, sg.ap, reads=[sg]))
            if b + 1 < NB:
                self.load_x_tile(b + 1, t)
        return (sf, sh_view, out_fn, post)

    def ffn(self, b, l, f, next_nd=None):
        AR, dr, o = self.AR, self.dr, self.o
        n = 0 if f == 0 else 2
        mb = self.modb(l, b)
        sc_ = self.scal(l, b)
        A_view = sc_.sub(sc_.ap[:, n * 8:(n + 1) * 8])
        sh_view = mb.sub(mb.ap[:, (3 * n) * 8:(3 * n + 1) * 8])
        hg = sc_.sub(sc_.ap[:, 24 + n * 8:24 + (n + 1) * 8])
        abuf = [[AR.view(o["oA"] + (j * T + t * TW) * 2, BF16, TW) for t in range(TT)] for j in range(8)]
        wgu = [AR.view(o["oB"] + i * 4096, BF16, 2048, (2, DC, 128)) for i in range(3)]
        wdb = [AR.view(o["oB"] + 12288 + i * 2048, BF16, 1024, (8, 128)) for i in range(2)]
        ngu = getattr(self, "_ngu", 0)
        nwd = getattr(self, "_nwd", 0)
        fc0 = 0
        GL = FSPLIT[-1]
        wall = [AR.view(o["oB"] + 16384 + dc * GL * 256, BF16, GL * 128, (GL, 128)) for dc in range(DC)]
        for gi, grp in enumerate(FSPLIT):
            lastg = (gi == len(FSPLIT) - 1)
            wts = {}

            def load_w(j):
                nonlocal ngu
                fc = fc0 + j
                key = (b, l, f, fc)
                if key in self.pre_wgu:
                    wts[j] = self.pre_wgu.pop(key)
                    return
                w = wgu[ngu % 3]
                ngu += 1
                self.dma("pool", w.ap.rearrange("p a b c -> p (a b c)"), dr["wgu"][l, f, fc], writes=[w])
                wts[j] = w
                if lastg and j == 0:
                    for dc in range(DC):
                        self.dma("pool", wall[dc].ap,
                                 dr["wd"][l, f, fc0:fc0 + grp, dc].rearrange("a p d -> p a d"), writes=[wall[dc]])

            def block(j, t):
                w = wts[j]
                pg = self.next_ps()
                pu = self.next_ps()
                for dc in range(DC):
                    self.mm(pg, w.sub(w.ap[:, 0, dc, :]), self.H[dc][t], dc == 0, dc == DC - 1)
                for dc in range(DC):
                    self.mm(pu, w.sub(w.ap[:, 1, dc, :]), self.H[dc][t], dc == 0, dc == DC - 1)
                sg = self.rot("tmp")
                self.act(sg, pg, AF.Silu)
                self.tt("dve", abuf[j][t], sg, pu, ALU.mult)

            def after_fc(j):
                fc = fc0 + j
                self.mod_step()
                if len(self.mod_pending) > NFC - 1 - fc:
                    self.mod_step()

            j_start = 0
            if gi == 0:
                load_w(0)
                load_w(1)
                for (j, t) in ((0, 0), (0, 1), (1, 0), (1, 1), (0, 2), (1, 2)):
                    block(j, t)
                    self.flush()
                self.flush(all_=True)
                for j in (0, 1):
                    block(j, TT - 1)
                    after_fc(j)
                j_start = 2
            for j in range(j_start, grp):
                load_w(j)
                for t in range(TT):
                    block(j, t)
                after_fc(j)
            if lastg:
                assert grp == GL
                for t in range(TT):
                    for dc in range(DC):
                        w = wall[dc]
                        py = self.next_ps()
                        for j in range(grp):
                            self.mm(py, w.sub(w.ap[:, j, :]), abuf[j][t], j == 0, j == grp - 1)
                        self.stt(self.X[dc][t], py, hg.ap[:, dc:dc + 1], self.X[dc][t], ALU.mult, ALU.add,
                                 extra_reads=[hg])
                        if dc == 4:
                            self.tail_mid(next_nd, t)
                    self.tail_end(next_nd, t)
                fc0 += grp
                continue
            for dc in range(DC):
                w = wdb[nwd % 2]
                nwd += 1
                wv = w.sub(w.ap[:, 0:grp, :])
                self.dma("pool", wv.ap,
                         dr["wd"][l, f, fc0:fc0 + grp, dc].rearrange("a p d -> p a d"), writes=[w])
                for t in range(TT):
                    py = self.next_ps()
                    for j in range(grp):
                        self.mm(py, w.sub(w.ap[:, j, :]), abuf[j][t], j == 0, j == grp - 1)
                    self.stt(self.X[dc][t], py, hg.ap[:, dc:dc + 1], self.X[dc][t], ALU.mult, ALU.add,
                             extra_reads=[hg])
            fc0 += grp
        self._ngu, self._nwd = ngu, nwd

    def mixer(self, b, l, next_nd=None):
        AR, dr, o, vcol = self.AR, self.dr, self.o, self.vcol
        oA, oB, oZ, oF = o["oA"], o["oB"], o["oZ"], o["oF"]
        mb = self.modb(l, b)
        sc_ = self.scal(l, b)
        g2 = sc_.sub(sc_.ap[:, 32:40])
        H, X = self.H, self.X

        VP = 2080
        vp = [AR.view(oA + cc * VP * 2, BF16, VP) for cc in range(4)]
        oD = oA + 4 * VP * 2
        oD = (oD + 255) // 256 * 256
        Dm = [AR.view(oD + i * 31 * 256, BF16, 31 * 128, (31, 128)) for i in range(2)]
        assert oD + 2 * 31 * 256 <= oA + 32768
        cv = [[AR.view(oB + (cc * T + t * TW) * 4, F32, TW) for t in range(TT)] for cc in range(4)]
        z = [[AR.view(oZ + (cc * T + t * TW) * 2, BF16, TW) for t in range(TT)] for cc in range(4)]
        wcf = [AR.view(oB + 24576 + i * 4096, BF16, 2048, (2, DC, 128)) for i in range(2)]
        assert 16384 + DC * FSPLIT[-1] * 256 <= 28672
        for cc in range(4):
            pad0 = vp[cc].sub(vp[cc].ap[:, 0:16])
            pad1 = vp[cc].sub(vp[cc].ap[:, VP - 16:VP])
            self.S.add("dve", lambda e, p=pad0: e.memset(p.ap, 0.0), writes=[pad0])
            self.S.add("dve", lambda e, p=pad1: e.memset(p.ap, 0.0), writes=[pad1])

        def build_D(cc):
            for k in range(31):
                dk = AR.view(oD + (cc % 2) * 31 * 256 + k * 256, BF16, 128)
                self.ts("dve", dk, self.ident, vcol("ccw", l, k * 4 + cc), None, ALU.mult,
                        extra_reads=[self.vec])
        wcf0 = AR.view(oB + 12288, BF16, 2048, (2, DC, 128))
        wconf = {}

        def load_conf(cc):
            w = wcf0 if cc == 0 else wcf[cc % 2]
            self.dma("pool", w.ap.rearrange("p a b c -> p (a b c)"), dr["wconf"][l, cc], writes=[w])
            wconf[cc] = w
        load_conf(0)
        load_conf(1)
        wft = AR.view(oZ, BF16, 4 * D, (4, D))
        self.dma("pool", wft.ap.rearrange("p a b -> p (a b)"), dr["wfT"][l], writes=[wft])
        c64, s64 = self.c64, self.s64

        def prep_w(Wo, cmat, d0, d1):
            for dcn in range(d0, d1):
                ps = self.next_ps()
                for cc in range(4):
                    lhs = wft.sub(wft.ap[:, cc, dcn * 128:(dcn + 1) * 128])
                    self.mm(ps, lhs, cmat, True, True, ps_ap=ps.ap[:, cc * 128:(cc + 1) * 128])
                wrow = AR.view(Wo + dcn * 1024, BF16, 512)
                self.S.add("dve", lambda e, o_=wrow, p_=ps: e.tensor_copy(out=o_.ap, in_=p_.ap),
                           reads=[ps], writes=[wrow])
        build_D(0)
        build_D(1)

        def proj(cc, t):
            w = wconf[cc]
            pa = self.next_ps()
            pb = self.next_ps()
            for dc in range(DC):
                self.mm(pa, w.sub(w.ap[:, 0, dc, :]), H[dc][t], dc == 0, dc == DC - 1)
            for dc in range(DC):
                self.mm(pb, w.sub(w.ap[:, 1, dc, :]), H[dc][t], dc == 0, dc == DC - 1)
            sg = self.rot("tmp")
            self.act(sg, pb, AF.Sigmoid)
            vt = AR.view(oA + (cc * VP + 16 + t * TW) * 2, BF16, TW)
            self.tt("dve", vt, sg, pa, ALU.mult)

        def conv(cc):
            for t in range(TT):
                pc = self.next_ps()
                for k in range(31):
                    dk = AR.view(oD + (cc % 2) * 31 * 256 + k * 256, BF16, 128)
                    rhs = AR.view(oA + (cc * VP + t * TW + k + 1) * 2, BF16, TW)
                    self.mm(pc, dk, rhs, k == 0, k == 30)
                self.act(cv[cc][t], pc, AF.Identity, bias=vcol("ccb", l, cc), extra_reads=[self.vec])

        for (cc, t) in ((0, 0), (0, 1), (1, 0), (1, 1), (0, 2), (1, 2)):
            proj(cc, t)
            self.flush()
        self.flush(all_=True)
        prep_w(oF, c64, 0, DC)
        proj(0, TT - 1)
        proj(1, TT - 1)
        conv(0)
        load_conf(2)
        conv(1)
        for t in range(TT):
            proj(2, t)
        build_D(2)
        load_conf(3)
        conv(2)
        for t in range(TT):
            proj(3, t)
        build_D(3)
        conv(3)
        lnr = [AR.view(oF + 8192 + i * 1024, BF16, TW) for i in range(8)]
        lnc = [0]

        def ln_tmp():
            v = lnr[lnc[0] % 8]
            lnc[0] += 1
            return v

        def ln_act(t):
            tm_ = []
            for cc in range(4):
                cb = ln_tmp()
                self.S.add("pool", lambda e, o_=cb, i_=cv[cc][t]: e.tensor_copy(out=o_.ap, in_=i_.ap),
                           reads=[cv[cc][t]], writes=[cb])
                tm_.append(cb)
            for cc in range(4):
                cq = ln_tmp()
                self.act(cq, cv[cc][t], AF.Square)
                tm_.append(cq)
            return tm_

        def ln_pe(tm_):
            p1 = self.next_ps()
            p2 = self.next_ps()
            for cc in range(4):
                self.mm(p1, self.ones, tm_[cc], cc == 0, cc == 3)
            for cc in range(4):
                self.mm(p2, self.ones, tm_[4 + cc], cc == 0, cc == 3)
            return p1, p2

        def ln_b(t, p1, p2):
            m = self.rot("rs")
            self.ts("dve", m, p1, 1.0 / 512, None, ALU.mult)
            var = self.rot("rs")
            self.tt("dve", var, m, m, ALU.mult)
            self.stt(var, p2, 1.0 / 512, var, ALU.mult, ALU.subtract)
            self.ts("dve", var, var, 0.0, EPS, ALU.max, ALU.add)
            self.act(var, var, AF.Sqrt)
            self.S.add("dve", lambda e, r=var: e.reciprocal(out=r.ap, in_=r.ap), reads=[var], writes=[var])
            for cc in range(4):
                tm = self.rot("tmp")
                self.tt("dve", tm, cv[cc][t], m, ALU.subtract)
                self.tt("dve", tm, tm, var, ALU.mult)
                self.act(z[cc][t], tm, AF.Silu, bias=vcol("lnb", l, cc), scale=vcol("lng", l, cc),
                         extra_reads=[self.vec])

        uc = [AR.view(oA + sc * 1024, BF16, 512) for sc in range(16)]
        us = [AR.view(oA + 16384 + sc * 1024, BF16, 512) for sc in range(16)]

        def g3(v, c0, n, rev=False):
            a_ = v.ap
            ps_ = a_.ap[0]
            if rev:
                return AP(tensor=a_.tensor, offset=a_.offset + c0 + n - 1, ap=[[ps_[0], ps_[1]], [64, 8], [-1, n]])
            return AP(tensor=a_.tensor, offset=a_.offset + c0, ap=[[ps_[0], ps_[1]], [64, 8], [1, n]])

        usall = AR.view(oA + 16384, BF16, 16 * 512)
        for c0 in (0, 32):
            a_ = usall.ap
            zap = AP(tensor=a_.tensor, offset=a_.offset + c0, ap=[[a_.ap[0][0], a_.ap[0][1]], [64, 128], [1, 1]])
            self.S.add("dve", lambda e, z_=zap: e.memset(z_, 0.0), writes=[usall])

        def uproj(sc0, sc1):
            for sc in range(sc0, sc1):
                t, q = sc // 4, sc % 4
                ps = self.next_ps()
                for dc in range(DC):
                    lhs = H[dc][t].sub(H[dc][t].ap[:, q * 128:(q + 1) * 128])
                    rhs = AR.view(oF + dc * 1024, BF16, 512)
                    self.mm(ps, lhs, rhs, dc == 0, dc == DC - 1)
                self.S.add("act", lambda e, o_=g3(uc[sc], 0, 33), i_=g3(ps, 0, 33):
                           e.activation(out=o_, in_=i_, func=AF.Copy), reads=[ps], writes=[uc[sc]])
                self.S.add("dve", lambda e, o_=g3(uc[sc], 33, 31, rev=True), i_=g3(ps, 1, 31):
                           e.tensor_copy(out=o_, in_=i_), reads=[ps], writes=[uc[sc]])
                self.S.add("dve", lambda e, o_=g3(us[sc], 1, 31), i_=g3(ps, 33, 31):
                           e.tensor_copy(out=o_, in_=i_), reads=[ps], writes=[us[sc]])
                self.S.add("act", lambda e, o_=g3(us[sc], 33, 31, rev=True), i_=g3(ps, 33, 31):
                           e.activation(out=o_, in_=i_, func=AF.Copy, scale=-1.0), reads=[ps], writes=[us[sc]])

        fill = []
        for sc0 in range(0, 16, 2):
            fill.append(lambda sc0=sc0: uproj(sc0, sc0 + 2))
        ta = ln_act(0)
        fill.pop(0)()
        pa = ln_pe(ta)
        for t in range(TT):
            if t + 1 < TT:
                ta = ln_act(t + 1)
            fill.pop(0)()
            ln_b(t, *pa)
            if t + 1 < TT:
                pa = ln_pe(ta)
        while fill:
            fill.pop(0)()
        csb = [AR.view(oB + i * 16384, BF16, 2 * 16 * KTW, (2, 16, KTW)) for i in range(2)]
        for kt in range(NKT):
            cb = csb[kt % 2]
            self.dma("sp", cb.ap.rearrange("p a b c -> p (a b c)"), dr["cs"][kt], writes=[cb])
            for cc in range(4):
                pe_ = self.next_ps()
                po_ = self.next_ps()
                pev = pe_.ap[:, 0:KTW]
                pov = po_.ap[:, 0:KTW]
                for sc in range(16):
                    self.mm(pe_, uc[sc].sub(uc[sc].ap[:, cc * 128:(cc + 1) * 128]),
                            cb.sub(cb.ap[:, 0, sc, :]), sc == 0, sc == 15, ps_ap=pev)
                for sc in range(16):
                    self.mm(po_, us[sc].sub(us[sc].ap[:, cc * 128:(cc + 1) * 128]),
                            cb.sub(cb.ap[:, 1, sc, :]), sc == 0, sc == 15, ps_ap=pov)
                esb = self.rot("tmp")
                ev = esb.sub(esb.ap[:, 0:KTW])
                self.act(ev, pe_.sub(pev), AF.Copy)
                fo = AR.view(oF + (cc * T + kt * KTW) * 2, BF16, KTW)
                self.tt("dve", fo, ev, po_.sub(pov), ALU.add)
                j0 = 1 if kt == 0 else 0
                n = KTW - j0
                hi = T - kt * KTW - j0
                fh = AR.view(oF + (cc * T + hi - (n - 1)) * 2, BF16, n)
                a_ = fh.ap
                rap = AP(tensor=a_.tensor, offset=a_.offset + (n - 1), ap=[[a_.ap[0][0], a_.ap[0][1]], [-1, n]])
                self.tt("dve", View(rap, fh.keys), esb.sub(esb.ap[:, j0:KTW]), po_.sub(po_.ap[:, j0:KTW]),
                        ALU.subtract)
        for cc in range(4):
            pn = self.next_ps()
            for sc in range(16):
                self.mm(pn, uc[sc].sub(uc[sc].ap[:, cc * 128:(cc + 1) * 128]),
                        self.csn.sub(self.csn.ap[:, sc:sc + 1]), sc == 0, sc == 15, ps_ap=pn.ap[:, 0:1])
            fn_ = AR.view(oF + (cc * T + T // 2) * 2, BF16, 1)
            self.act(fn_, pn.sub(pn.ap[:, 0:1]), AF.Copy)

        mg = [[AR.view(oA + (dc * T + t * TW) * 2, BF16, TW) for t in range(TT)] for dc in range(DC)]
        wmgb = [AR.view(oB + 16640 + i * 3072, BF16, 12 * 128, (12, 128)) for i in range(3)]
        nmg = getattr(self, "_nmg", 0)

        def merge(i, src, first, after_dma2=None):
            nonlocal nmg
            for dc in range(DC):
                w = wmgb[nmg % 3]
                nmg += 1
                self.dma("pool", w.ap.rearrange("p a b -> p (a b)"), dr["wmg"][l, i, dc], writes=[w])
                if dc == 2 and after_dma2 is not None:
                    after_dma2()
                for t in range(TT):
                    pg = self.next_ps()
                    py = self.next_ps()
                    for di in range(DC):
                        self.mm(pg, w.sub(w.ap[:, di, :]), H[di][t], di == 0, di == DC - 1)
                    for cc in range(4):
                        self.mm(py, w.sub(w.ap[:, 8 + cc, :]), src(cc, t), cc == 0, cc == 3)
                    sg = self.rot("tmp")
                    self.act(sg, pg, AF.Sigmoid, bias=vcol("bgate", l, i * 8 + dc), extra_reads=[self.vec])
                    if first:
                        self.tt("dve", mg[dc][t], sg, py, ALU.mult)
                    else:
                        self.tt("dve", sg, sg, py, ALU.mult)
                        self.tt("dve", mg[dc][t], mg[dc][t], sg, ALU.add)

        merge(0, lambda cc, t: AR.view(oF + (cc * T + t * TW) * 2, BF16, TW), True)
        merge(2, lambda cc, t: z[cc][t], False)

        PP = 2052
        pp = [AR.view(oB + cc * PP * 2, BF16, PP) for cc in range(4)]
        assert 4 * PP * 2 <= 16640
        r = [[AR.view(oZ + (cc * T + t * TW) * 2, BF16, TW) for t in range(TT)] for cc in range(4)]
        d3 = [AR.view(oF + i * 256, BF16, 128) for i in range(12)]
        wsh = [AR.view(oF + 3072 + i * 6144, BF16, 3 * DC * 128, (3, DC, 128)) for i in range(2)]
        for cc in range(4):
            pad0 = pp[cc].sub(pp[cc].ap[:, 0:2])
            pad1 = pp[cc].sub(pp[cc].ap[:, PP - 2:PP])
            self.S.add("dve", lambda e, p=pad0: e.memset(p.ap, 0.0), writes=[pad0])
            self.S.add("dve", lambda e, p=pad1: e.memset(p.ap, 0.0), writes=[pad1])
            for k in range(3):
                self.ts("dve", d3[k * 4 + cc], self.ident, vcol("csw", l, k * 4 + cc), None, ALU.mult,
                        extra_reads=[self.vec])
        for cc in range(4):
            w = wsh[cc % 2]
            self.dma("pool", w.ap.rearrange("p a b c -> p (a b c)"), dr["wshort"][l, cc], writes=[w])
            for t in range(TT):
                p1 = self.next_ps()
                p2 = self.next_ps()
                for dc in range(DC):
                    self.mm(p1, w.sub(w.ap[:, 0, dc, :]), H[dc][t], dc == 0, dc == DC - 1)
                for dc in range(DC):
                    self.mm(p2, w.sub(w.ap[:, 1, dc, :]), H[dc][t], dc == 0, dc == DC - 1)
                tm = self.rot("tmp")
                self.act(tm, p1, AF.Copy)
                pt = AR.view(oB + (cc * PP + 2 + t * TW) * 2, BF16, TW)
                self.tt("dve", pt, tm, p2, ALU.mult)
            for t in range(TT):
                pq = self.next_ps()
                pbg = self.next_ps()
                for k in range(3):
                    rhs = AR.view(oB + (cc * PP + t * TW + k + 1) * 2, BF16, TW)
                    self.mm(pq, d3[k * 4 + cc], rhs, k == 0, k == 2)
                for dc in range(DC):
                    self.mm(pbg, w.sub(w.ap[:, 2, dc, :]), H[dc][t], dc == 0, dc == DC - 1)
                tm = self.rot("tmp")
                self.act(tm, pq, AF.Copy)
                self.tt("dve", r[cc][t], tm, pbg, ALU.mult)
        woall = [AR.view(oF + i * 2048, BF16, DC * 128, (DC, 128)) for i in range(DC)]

        def load_wo():
            for dcn in range(DC):
                self.dma("pool", woall[dcn].ap.rearrange("p a b -> p (a b)"), dr["wo"][l, dcn], writes=[woall[dcn]])
        merge(1, lambda cc, t: r[cc][t], False, after_dma2=load_wo)
        self._nmg = nmg

        for t in range(TT):
            for dcn in range(DC):
                w = woall[dcn]
                po = self.next_ps()
                for dc in range(DC):
                    self.mm(po, w.sub(w.ap[:, dc, :]), mg[dc][t], dc == 0, dc == DC - 1)
                self.stt(X[dcn][t], po, g2.ap[:, dcn:dcn + 1], X[dcn][t], ALU.mult, ALU.add,
                         extra_reads=[g2])
                if dcn == 4:
                    self.tail_mid(next_nd, t)
            self.tail_end(next_nd, t)

    def finish(self, b, outT):
        AR, o = self.AR, self.o
        ops = []
        oH = 65536
        stg = [AR.view(oH + i * 16384, F32, DC * TW, (DC, TW)) for i in range(2)]
        if self.final:
            sf = self.scalfin(b)
            mf = self.modfin(b)
        for t in range(TT):
            sg = stg[t % 2]
            if self.final:
                r = self.rstd_tile(t)
            for dc in range(DC):
                ot = AR.view(oH + (t % 2) * 16384 + dc * TW * 4, F32, TW)
                if self.final:
                    tm = self.rot("tmp")
                    self.stt(tm, self.X[dc][t], sf.ap[:, dc:dc + 1], r, ALU.mult, ALU.mult, extra_reads=[sf])
                    self.act(ot, tm, AF.Identity, bias=mf.ap[:, dc:dc + 1], extra_reads=[mf])
                else:
                    self.act(ot, self.X[dc][t], AF.Copy)
            dst = outT[b].rearrange("(a p) t -> p a t", p=128)[:, :, t * TW:(t + 1) * TW]
            ops.append(self.dma("sp", dst, sg.ap, reads=[sg]))
        return ops


def _consts():
    s = np.arange(T, dtype=np.float64)
    ang = 2.0 * np.pi * np.outer(s, s) / T
    C = np.cos(ang) / np.sqrt(T)
    Sn = -np.sin(ang) / np.sqrt(T)
    cs = np.stack([C, Sn], 0)[:, :, :NKT * KTW].reshape(2, 16, 128, NKT, KTW).transpose(3, 2, 0, 1, 4)
    cs = np.ascontiguousarray(cs).reshape(NKT, 128, 2 * 16 * KTW).astype(ml_dtypes.bfloat16)
    nyq = np.zeros((128, 128))
    nyq[:, 0:16] = C[:, T // 2].reshape(16, 128).T
    j = np.arange(64, dtype=np.float64)
    a64 = 2.0 * np.pi * np.outer(j, j) / 64
    c64 = np.zeros((128, 128))
    s64 = np.zeros((128, 128))
    for g in range(2):
        c64[g * 64:(g + 1) * 64, g * 64:(g + 1) * 64] = np.cos(a64) / 8.0
        s64[g * 64:(g + 1) * 64, g * 64:(g + 1) * 64] = np.sin(a64) / 8.0
    sel = np.zeros((128, 128))
    for g in range(2):
        sel[:, g * 64:g * 64 + 33] = c64[:, g * 64:g * 64 + 33]
        sel[:, g * 64 + 33:g * 64 + 64] = s64[:, g * 64 + 1:g * 64 + 32]
    cst = np.stack([np.eye(128), np.ones((128, 128)), sel, s64, nyq], 1).astype(ml_dtypes.bfloat16)
    return cs, np.ascontiguousarray(cst)


def _col(v):
    return np.ascontiguousarray(np.asarray(v, np.float32).reshape(-1, 128).T)


def _prep_shared(inp):
    f = lambda k: np.asarray(inp[k], np.float32)
    sh = {}
    vec = np.zeros((128, NVEC), np.float32)
    for l in range(DEPTH):
        def put(name, arr):
            o = VOFF[(name, l)]
            vec[:, o:o + arr.shape[1]] = arr
        put("g0", _col(f("ffn1_norm_g")[l]))
        put("g1", _col(f("mix_norm_g")[l]))
        put("g2", _col(f("ffn2_norm_g")[l]))
        put("bmod", _col(f("b_mod")[l]))
        csw = f("conv_short_w")[l].reshape(3, 4, 128).transpose(2, 0, 1).reshape(128, 12)
        put("csw", csw)
        ccw = f("conv_conf_w")[l].reshape(31, 4, 128).transpose(2, 0, 1).reshape(128, 124)
        put("ccw", ccw)
        put("ccb", _col(f("conv_conf_b")[l]))
        put("lng", _col(f("conf_ln_g")[l]))
        put("lnb", _col(f("conf_ln_b")[l]))
        put("bgate", _col(f("b_gate")[l]))
    o = VOFF[("gf", 0)]
    vec[:, o:o + 8] = _col(f("final_norm_g"))
    o = VOFF[("bfin", 0)]
    vec[:, o:o + 16] = _col(f("b_final_mod"))
    o = VOFF[("eye2", 0)]
    vec[0, o] = 1.0
    vec[1, o + 1] = 1.0
    sh["vec"] = vec
    cs, cst = _consts()
    sh["cs"] = cs
    sh["cst"] = cst
    wm = f("w_mod").reshape(DEPTH, DC, 128, 18, 512).transpose(0, 3, 2, 1, 4).reshape(DEPTH * 18, 128, DC * 512)
    wf = f("w_final_mod").reshape(DC, 128, 4, 512).transpose(2, 1, 0, 3).reshape(4, 128, DC * 512)
    sh["wmodt"] = np.ascontiguousarray(np.concatenate([wm, wf], 0))
    def gu(k):
        return f(k).reshape(DEPTH, DC, 128, NFC, 128).transpose(0, 3, 2, 1, 4)
    wgu = np.stack([np.stack([gu("ffn1_w_gate"), gu("ffn1_w_up")], 3),
                    np.stack([gu("ffn2_w_gate"), gu("ffn2_w_up")], 3)], 1)
    sh["wgu"] = np.ascontiguousarray(wgu).reshape(DEPTH, 2, NFC, 128, 2 * DC * 128)
    def dn(k):
        return f(k).reshape(DEPTH, NFC, 128, DC, 128).transpose(0, 1, 3, 2, 4)
    sh["wd"] = np.ascontiguousarray(np.stack([dn("ffn1_w_down"), dn("ffn2_w_down")], 1))
    w_in = f("w_in")
    def colgrp(c0):
        return w_in[:, :, c0:c0 + 512].reshape(DEPTH, DC, 128, 4, 128).transpose(0, 3, 2, 1, 4)
    u_bg, u_cg, u_x, u_ga, u_gb = (colgrp(512), colgrp(1024), colgrp(1536), colgrp(2048), colgrp(2560))
    sh["wconf"] = np.ascontiguousarray(np.stack([u_ga, u_gb], 3)).reshape(DEPTH, 4, 128, 2 * DC * 128)
    sh["wshort"] = np.ascontiguousarray(np.stack([u_cg, u_x, u_bg], 3)).reshape(DEPTH, 4, 128, 3 * DC * 128)
    wfT = w_in[:, :, 0:512].transpose(0, 2, 1).reshape(DEPTH, 4, 128, D).transpose(0, 2, 1, 3)
    sh["wfT"] = np.ascontiguousarray(wfT).reshape(DEPTH, 128, 4 * D)
    wg = f("w_gate").reshape(DEPTH, DC, 128, 3, DC, 128).transpose(0, 3, 4, 2, 1, 5)
    br = np.stack([f("w_branch_f"), f("w_branch_s"), f("w_branch_c")], 1)
    br = br.reshape(DEPTH, 3, 4, 128, DC, 128).transpose(0, 1, 4, 3, 2, 5)
    sh["wmg"] = np.ascontiguousarray(np.concatenate([wg, br], 4)).reshape(DEPTH, 3, DC, 128, 12 * 128)
    wo = f("w_out").reshape(DEPTH, DC, 128, DC, 128).transpose(0, 3, 2, 1, 4)
    sh["wo"] = np.ascontiguousarray(wo).reshape(DEPTH, DC, 128, DC * 128)
    return sh


_NC_CACHE = {}


def _get_nc(nstage=3 * DEPTH, final=True):
    key = (nstage, final)
    if key not in _NC_CACHE:
        _NC_CACHE[key] = Builder(nstage, final).build()
    return _NC_CACHE[key]


def kernel(_nstage=3 * DEPTH, _final=True, _ncores=8, **inp):
    x = np.asarray(inp["x"], np.float32)
    c = np.asarray(inp["c"], np.float32)
    sh = _prep_shared(inp)
    nc = _get_nc(_nstage, _final)
    in_maps = []
    for i in range(_ncores):
        xb = x[i * NB:(i + 1) * NB]
        m = dict(sh)
        m["xT"] = np.ascontiguousarray(xb.transpose(0, 2, 1))
        cb = c[i * NB:(i + 1) * NB]
        m["cT"] = np.ascontiguousarray(cb.reshape(NB, DC, 128).transpose(2, 1, 0))
        in_maps.append(m)
    res = run_bass_kernel_spmd(nc, in_maps, core_ids=list(range(_ncores)))
    outs = [np.asarray(r["outT"]).transpose(0, 2, 1) for r in res.results]
    return np.ascontiguousarray(np.concatenate(outs, 0)).astype(np.float32)
```

```python
import contextlib
import numpy as np
import ml_dtypes
import concourse.bass as bass
import concourse.mybir as mybir
from concourse.bass_utils import run_bass_kernel_spmd
from concourse.ap import AP

F32 = mybir.dt.float32
BF16 = mybir.dt.bfloat16
U8 = mybir.dt.uint8
AF = mybir.ActivationFunctionType
ALU = mybir.AluOpType

D = 1024
T = 2048
NB = 2
DEPTH = 2
DFF = 2816
NFC = 22
DC = 8
TT = 4
TW = 512
EPS = 1e-6
FSPLIT = (8, 8, 6)
NMT = 18 * DEPTH + 4
NKT = 4
KTW = 256

ENGS = ("pe", "act", "dve", "pool", "sp")
BLK = 256


class View:
    __slots__ = ("ap", "keys", "gen")

    def __init__(self, ap, keys, gen=None):
        self.ap = ap
        self.keys = keys
        self.gen = gen

    def sub(self, ap):
        return View(ap, self.keys, self.gen)


class Op:
    __slots__ = ("eng", "fn", "reads", "writes", "is_dma", "eidx", "waits",
                 "signal", "sem", "val", "prewait")

    def __init__(self, eng, fn, reads, writes, is_dma):
        self.eng = eng
        self.fn = fn
        self.reads = reads
        self.writes = writes
        self.is_dma = is_dma
        self.waits = []
        self.signal = False
        self.sem = None
        self.val = 0
        self.prewait = None


class Sched:
    def __init__(self, dma_ring=6):
        self.ops = []
        self.eng_ops = {e: [] for e in ENGS}
        self.K = dma_ring
        self.ps_cur = {}

    def add(self, eng, fn, reads=(), writes=(), dma=False):
        rk = []
        for v in reads:
            rk.extend(v.keys)
            if v.gen is not None:
                assert self.ps_cur[v.gen % 8] == v.gen, "PSUM bank recycled before its reader"
        wk = []
        for v in writes:
            wk.extend(v.keys)
        op = Op(eng, fn, rk, wk, dma)
        op.eidx = len(self.eng_ops[eng])
        self.eng_ops[eng].append(op)
        self.ops.append(op)
        return op

    def analyze(self):
        last_writer = {}
        readers = {}
        known = {e: {} for e in ENGS}
        known_dma = {e: set() for e in ENGS}
        for op in self.ops:
            deps = {}
            for k in op.reads:
                lw = last_writer.get(k)
                if lw is not None and lw is not op:
                    deps[id(lw)] = (lw, True)
            for k in op.writes:
                lw = last_writer.get(k)
                if lw is not None and lw is not op and id(lw) not in deps:
                    deps[id(lw)] = (lw, False)
                rl = readers.get(k)
                if rl:
                    for rd in rl:
                        if rd is not op and id(rd) not in deps:
                            deps[id(rd)] = (rd, False)
            best = {}
            for d, is_raw in deps.values():
                if (not d.is_dma) and (not op.is_dma) and d.eng == op.eng:
                    if d.eng == "pe":
                        continue
                if d.is_dma:
                    if id(d) in known_dma[op.eng]:
                        continue
                    known_dma[op.eng].add(id(d))
                    d.signal = True
                    op.waits.append(d)
                else:
                    cur = best.get(d.eng)
                    if cur is None or d.eidx > cur.eidx:
                        best[d.eng] = d
            for d in best.values():
                if known[op.eng].get(d.eng, -1) >= d.eidx:
                    continue
                known[op.eng][d.eng] = d.eidx
                d.signal = True
                op.waits.append(d)
            for k in op.reads:
                rl = readers.get(k)
                if rl is None:
                    readers[k] = [op]
                elif not rl or rl[-1] is not op:
                    rl.append(op)
            for k in op.writes:
                last_writer[k] = op
                readers[k] = []

    def emit(self, nc, final_wait_ops=()):
        with contextlib.ExitStack() as st:
            esem = {e: st.enter_context(nc.semaphore("s_" + e)) for e in ENGS}
            dsem = {}
            for e in ENGS:
                if any(o.is_dma for o in self.eng_ops[e]):
                    dsem[e] = [st.enter_context(nc.semaphore("d_%s%d" % (e, i)))
                               for i in range(self.K)]
            for e in ENGS:
                cnt = 0
                j = 0
                for op in self.eng_ops[e]:
                    if op.is_dma:
                        op.sem = dsem[e][j % self.K]
                        op.val = 16 * (j // self.K + 1)
                        if j >= self.K:
                            op.prewait = (op.sem, 16 * (j // self.K))
                        j += 1
                    elif op.signal:
                        cnt += 1
                        op.sem = esem[e]
                        op.val = cnt
            block = st.enter_context(nc.Block())

            def make(e):
                def body(eng):
                    for op in self.eng_ops[e]:
                        if op.prewait is not None:
                            eng.wait_ge(op.prewait[0], op.prewait[1])
                        for d in op.waits:
                            eng.wait_ge(d.sem, d.val)
                        if op.fn is None:
                            continue
                        ins = op.fn(eng)
                        if op.is_dma:
                            ins.then_inc(op.sem, 16)
                        elif op.signal:
                            ins.then_inc(op.sem, 1)
                    if e == "sp":
                        for o in final_wait_ops:
                            eng.wait_ge(o.sem, o.val)
                return body

            block.tensor(make("pe"))
            block.scalar(make("act"))
            block.vector(make("dve"))
            block.gpsimd(make("pool"))
            block.sync(make("sp"))


class Arena:
    def __init__(self, nbytes):
        self.nbytes = nbytes
        self.t = None

    def view(self, off, dtype, ncols, shape=None):
        esz = 4 if dtype == F32 else 2
        nb = ncols * esz
        assert off % esz == 0 and off >= 0 and off + nb <= self.nbytes, (off, nb, self.nbytes)
        ap = self.t[:, off:off + nb].bitcast(dtype)
        if shape is not None:
            if len(shape) == 2:
                ap = ap.rearrange("p (a b) -> p a b", a=shape[0], b=shape[1])
            elif len(shape) == 3:
                ap = ap.rearrange("p (a b c) -> p a b c", a=shape[0], b=shape[1], c=shape[2])
        return View(ap, range(off // BLK, (off + nb - 1) // BLK + 1))


def _vec_layout():
    off = {}
    o = 0
    for l in range(DEPTH):
        for name, n in (("g0", 8), ("g1", 8), ("g2", 8), ("bmod", 72), ("csw", 12),
                        ("ccw", 124), ("ccb", 4), ("lng", 4), ("lnb", 4), ("bgate", 24)):
            off[(name, l)] = o
            o += n
    off[("gf", 0)] = o
    o += 8
    off[("bfin", 0)] = o
    o += 16
    off[("eye2", 0)] = o
    o += 2
    return off, o


VOFF, NVEC = _vec_layout()


class Builder:
    def __init__(self, nstage=3 * DEPTH, final=True):
        self.nstage = nstage
        self.final = final
        self.S = Sched()
        self.psi = 0

    def next_ps(self):
        base = self.PS[self.psi % 8]
        v = View(base.ap, base.keys, self.psi)
        self.S.ps_cur[self.psi % 8] = self.psi
        self.psi += 1
        return v

    def mm(self, ps, lhsT, rhs, start, stop, ps_ap=None):
        o = ps.ap if ps_ap is None else ps_ap
        self.S.add("pe", lambda e: e.matmul(o, lhsT=lhsT.ap, rhs=rhs.ap, start=start, stop=stop),
                   reads=[lhsT, rhs], writes=[ps])

    def act(self, out, in_, func, bias=None, scale=None, extra_reads=()):
        kw = {}
        if bias is not None:
            kw["bias"] = bias
        if scale is not None:
            kw["scale"] = scale
        self.S.add("act", lambda e: e.activation(out=out.ap, in_=in_.ap, func=func, **kw),
                   reads=[in_] + list(extra_reads), writes=[out])

    def tt(self, eng, out, in0, in1, op):
        self.S.add(eng, lambda e: e.tensor_tensor(out=out.ap, in0=in0.ap, in1=in1.ap, op=op),
                   reads=[in0, in1], writes=[out])

    def ts(self, eng, out, in0, s1, s2, op0, op1=None, extra_reads=()):
        if op1 is None:
            fn = lambda e: e.tensor_scalar(out=out.ap, in0=in0.ap, scalar1=s1, scalar2=None, op0=op0)
        else:
            fn = lambda e: e.tensor_scalar(out=out.ap, in0=in0.ap, scalar1=s1, scalar2=s2, op0=op0, op1=op1)
        self.S.add(eng, fn, reads=[in0] + list(extra_reads), writes=[out])

    def stt(self, out, in0, scalar, in1, op0, op1, extra_reads=()):
        self.S.add("dve", lambda e: e.scalar_tensor_tensor(out=out.ap, in0=in0.ap, scalar=scalar,
                                                           in1=in1.ap, op0=op0, op1=op1),
                   reads=[in0, in1] + list(extra_reads), writes=[out])

    def dma(self, q, out_ap, in_ap, reads=(), writes=()):
        return self.S.add(q, lambda e: e.dma_start(out=out_ap, in_=in_ap), reads=reads, writes=writes, dma=True)

    def build(self):
        nc = bass.Bass("TRN2", target_bir_lowering=False)
        self.nc = nc
        dr = {}

        def din(name, shape, dt=F32):
            dr[name] = nc.dram_tensor(name, list(shape), dt, kind="ExternalInput").ap()

        din("xT", [NB, D, T])
        din("cT", [128, DC, NB])
        din("vec", [128, NVEC])
        din("cst", [128, 5, 128], BF16)
        din("cs", [NKT, 128, 2 * 16 * KTW], BF16)
        din("wmodt", [NMT, 128, DC * 512])
        din("wgu", [DEPTH, 2, NFC, 128, 2 * DC * 128])
        din("wd", [DEPTH, 2, NFC, DC, 128, 128])
        din("wconf", [DEPTH, 4, 128, 2 * DC * 128])
        din("wshort", [DEPTH, 4, 128, 3 * DC * 128])
        din("wfT", [DEPTH, 128, 4 * D])
        din("wmg", [DEPTH, 3, DC, 128, 12 * 128])
        din("wo", [DEPTH, DC, 128, DC * 128])
        outT = nc.dram_tensor("outT", [NB, D, T], F32, kind="ExternalOutput").ap()
        self.dr = dr

        oX = 0
        oH = oX + 65536
        oA = oH + 32768
        oB = oA + 32768
        oZ = oB + 32768
        oF = oZ + 16384
        oT = oF + 16384
        T_SQB = oT
        T_RS = T_SQB + 2048
        T_TMP = T_RS + 4096
        oC = T_TMP + 4096
        C_CST = oC
        C_VEC = C_CST + 1280
        C_CACT = C_VEC + 4 * NVEC
        C_MODB = C_CACT + 64
        n_modb = DEPTH * NB * 72 + NB * 16
        C_SCAL = C_MODB + 4 * n_modb
        n_scal = DEPTH * NB * 48 + NB * 8
        total = C_SCAL + 4 * n_scal
        total = (total + 63) // 64 * 64
        assert total <= 212992, total
        AR = Arena(total)
        self.AR = AR

        with contextlib.ExitStack() as st:
            AR.t = st.enter_context(nc.sbuf_tensor("arena", [128, total], U8))
            pst = [st.enter_context(nc.psum_tensor("ps%d" % i, [128, 512], F32)) for i in range(8)]
            self.PS = [View(pst[i][:, :], [("ps", i)]) for i in range(8)]

            X = [[AR.view(oX + (dc * T + t * TW) * 4, F32, TW) for t in range(TT)] for dc in range(DC)]
            Xrow = [AR.view(oX + dc * T * 4, F32, T) for dc in range(DC)]
            H = [[AR.view(oH + (dc * T + t * TW) * 2, BF16, TW) for t in range(TT)] for dc in range(DC)]
            self.X, self.H = X, H
            cst = AR.view(C_CST, BF16, 640, (5, 128))
            ident = cst.sub(cst.ap[:, 0, :])
            ones = cst.sub(cst.ap[:, 1, :])
            c64 = cst.sub(cst.ap[:, 2, :])
            s64 = cst.sub(cst.ap[:, 3, :])
            self.csn = cst.sub(cst.ap[:, 4, 0:16])
            self.ident, self.ones, self.c64, self.s64 = ident, ones, c64, s64
            vec = AR.view(C_VEC, F32, NVEC)
            self.vec = vec
            sqb = [AR.view(T_SQB + i * 1024, BF16, TW) for i in range(2)]
            rs = [AR.view(T_RS + i * 2048, F32, TW) for i in range(2)]
            tmp = [AR.view(T_TMP + i * 2048, F32, TW) for i in range(2)]
            self.sqb, self.rs, self.tmp = sqb, rs, tmp
            self.cnt = {"sqb": 0, "rs": 0, "tmp": 0}

            def vcol(name, l, j):
                o = VOFF[(name, l)] + j
                return vec.ap[:, o:o + 1]
            self.vcol = vcol

            def modb(l, b):
                return AR.view(C_MODB + 4 * ((l * NB + b) * 72), F32, 72)

            def modfin(b):
                return AR.view(C_MODB + 4 * (DEPTH * NB * 72 + b * 16), F32, 16)

            def scal(l, b):
                return AR.view(C_SCAL + 4 * ((l * NB + b) * 48), F32, 48)

            def scalfin(b):
                return AR.view(C_SCAL + 4 * (DEPTH * NB * 48 + b * 8), F32, 8)
            self.modb, self.modfin, self.scal, self.scalfin = modb, modfin, scal, scalfin

            self.dma("sp", cst.ap, dr["cst"], writes=[cst])
            self.dma("sp", vec.ap, dr["vec"], writes=[vec])
            craw = AR.view(oT, F32, 16, (DC, NB))
            self.dma("sp", craw.ap, dr["cT"], writes=[craw])
            cact16 = AR.view(C_CACT, BF16, 16, (DC, NB))
            self.act(cact16, craw, AF.Silu)
            self.cact16 = cact16
            self.load_x(0)
            self.o = dict(oA=oA, oB=oB, oZ=oZ, oF=oF)
            self.mod_pending = list(range(NMT))
            self.pre_wgu = {}
            self.mod_step()
            self.mod_step()
            if self.nstage > 0:
                for fc in range(3):
                    w = AR.view(oB + fc * 4096, BF16, 2048, (2, DC, 128))
                    self.dma("pool", w.ap.rearrange("p a b c -> p (a b c)"), dr["wgu"][0, 0, fc], writes=[w])
                    self.pre_wgu[(0, 0, 0, fc)] = w
                self._ngu = 3
            for _ in range(4):
                self.mod_step()

            out_ops = []
            self.out_ops = out_ops
            self.outT = outT
            self.deferred = []
            self.sqz = [AR.view(oZ + i * 1024, BF16, TW) for i in range(16)]
            self._r = {}
            full = (self.nstage == 3 * DEPTH and self.final)
            for b in range(NB):
                if b > 0 and not full:
                    self.load_x(b)
                phases = []
                for l in range(DEPTH):
                    phases += [("ffn", l, 0), ("mix", l, 1), ("ffn", l, 1)]
                phases = phases[:self.nstage]
                descs = [self.phase_desc(b, p) for p in phases]
                if full:
                    descs.append(self.final_desc(b))
                else:
                    descs.append(None)
                if descs[0] is not None:
                    self.norm_start(descs[0])
                for i, p in enumerate(phases):
                    if p[0] == "mix":
                        self.mixer(b, p[1], descs[i + 1])
                    else:
                        self.ffn(b, p[1], p[2], descs[i + 1])
                    if b == 0 and i == 0:
                        self.mod_finish()
                if b == 0 and not phases:
                    self.mod_finish()
                self.flush(all_=True)
                if not full:
                    out_ops += self.finish(b, outT)
            self.S.analyze()
            self.S.emit(nc, final_wait_ops=out_ops)
        return nc

    def mod_step(self):
        if not self.mod_pending:
            return
        i = self.mod_pending.pop(0)
        AR, vec = self.AR, self.vec
        if i < 18 * DEPTH:
            l, ct = i // 18, i % 18
            dst = [self.modb(l, b) for b in range(NB)]
            bo = VOFF[("bmod", l)]
        else:
            ct = i - 18 * DEPTH
            dst = [self.modfin(b) for b in range(NB)]
            bo = VOFF[("bfin", 0)]
        slot = AR.view(self.o["oF"] + (i % 2) * 8192, BF16, DC * 512, (DC, 512))
        self.dma("pool", slot.ap.rearrange("p a b -> p (a b)"), self.dr["wmodt"][i], writes=[slot])
        ps = self.next_ps()
        for dc in range(DC):
            self.mm(ps, self.cact16.sub(self.cact16.ap[:, dc, :]), slot.sub(slot.ap[:, dc, :]),
                    dc == 0, dc == DC - 1, ps_ap=ps.ap[0:NB, :])
        row = self.rot("rs")
        rowv = row.sub(row.ap[0:NB, :])
        self.S.add("dve", lambda e: e.tensor_copy(out=rowv.ap, in_=ps.ap[0:NB, :]), reads=[ps], writes=[row])
        pt = self.next_ps()
        eo = VOFF[("eye2", 0)]
        eye = vec.sub(vec.ap[0:NB, eo:eo + NB])
        for q in range(4):
            self.mm(pt, row.sub(row.ap[0:NB, q * 128:(q + 1) * 128]), eye, True, True,
                    ps_ap=pt.ap[:, NB * q:NB * q + NB])
        ptv = pt.ap[:, 0:4 * NB].rearrange("p (q b) -> p q b", b=NB)
        for b in range(NB):
            d_ = dst[b]
            self.tt("dve", d_.sub(d_.ap[:, ct * 4:ct * 4 + 4]), pt.sub(ptv[:, :, b]),
                    vec.sub(vec.ap[:, bo + ct * 4:bo + ct * 4 + 4]), ALU.add)
        if i < 18 * DEPTH and ct % 6 == 5:
            self.derive(i // 18, ct // 6)
        if i == NMT - 1:
            self.derive_final()

    def derive(self, l, n_only=None):
        vec = self.vec
        for b in range(NB):
            mb = self.modb(l, b)
            sc_ = self.scal(l, b)
            for n in range(3):
                if n_only is not None and n != n_only:
                    continue
                a_out = sc_.sub(sc_.ap[:, n * 8:(n + 1) * 8])
                scl = mb.sub(mb.ap[:, (3 * n + 1) * 8:(3 * n + 2) * 8])
                gn = vec.sub(vec.ap[:, VOFF[("g%d" % n, l)]:VOFF[("g%d" % n, l)] + 8])
                self.stt(a_out, scl, 1.0, gn, ALU.add, ALU.mult)
                g_out = sc_.sub(sc_.ap[:, 24 + n * 8:24 + (n + 1) * 8])
                gt = mb.sub(mb.ap[:, (3 * n + 2) * 8:(3 * n + 3) * 8])
                self.ts("dve", g_out, gt, 1.0 if n == 1 else 0.5, None, ALU.mult)

    def derive_final(self):
        vec = self.vec
        for b in range(NB):
            mb = self.modfin(b)
            sf = self.scalfin(b)
            gf = vec.sub(vec.ap[:, VOFF[("gf", 0)]:VOFF[("gf", 0)] + 8])
            self.stt(sf, mb.sub(mb.ap[:, 8:16]), 1.0, gf, ALU.add, ALU.mult)

    def mod_finish(self):
        while self.mod_pending:
            self.mod_step()

    def load_x_tile(self, b, t):
        for dc in range(DC):
            xr = self.X[dc][t]
            self.dma("sp", xr.ap, self.dr["xT"][b, dc * 128:(dc + 1) * 128, t * TW:(t + 1) * TW], writes=[xr])

    def load_x(self, b):
        for t in range(TT):
            self.load_x_tile(b, t)

    def rot(self, name):
        lst = getattr(self, name)
        i = self.cnt[name]
        self.cnt[name] = i + 1
        return lst[i % len(lst)]

    def rstd_tile(self, t):
        self.norm_squares(t)
        return self.norm_stats(t)

    def norm_squares(self, t):
        for dc in range(DC):
            self.act(self.sqz[(t * DC + dc) % 16], self.X[dc][t], AF.Square)

    def norm_stats(self, t):
        ps = self.next_ps()
        for dc in range(DC):
            self.mm(ps, self.ones, self.sqz[(t * DC + dc) % 16], dc == 0, dc == DC - 1)
        r = self.rot("rs")
        self.ts("dve", r, ps, 1.0 / D, EPS, ALU.mult, ALU.add)
        self.act(r, r, AF.Sqrt)
        self.S.add("dve", lambda e: e.reciprocal(out=r.ap, in_=r.ap), reads=[r], writes=[r])
        self._r[t] = r
        return r

    def norm_apply(self, nd, t):
        A_view, sh_view, out_fn, post = nd
        r = self._r[t]
        for dc in range(DC):
            tm = self.rot("tmp")
            self.stt(tm, self.X[dc][t], A_view.ap[:, dc:dc + 1], r, ALU.mult, ALU.mult,
                     extra_reads=[A_view])
            self.act(out_fn(dc, t), tm, AF.Identity, bias=sh_view.ap[:, dc:dc + 1],
                     extra_reads=[sh_view])
        if post is not None:
            post(t)

    def norm_full(self, nd):
        for t in range(TT):
            self.norm_squares(t)
            self.norm_stats(t)
            self.norm_apply(nd, t)

    def tail_mid(self, nd, t):
        if nd is None or t == 0:
            return
        self.norm_stats(t - 1)
        self.norm_apply(nd, t - 1)

    def tail_end(self, nd, t):
        if nd is None:
            return
        self.norm_squares(t)
        if t == TT - 1:
            def last():
                self.norm_stats(TT - 1)
                self.norm_apply(nd, TT - 1)
            self.deferred.append(last)

    def flush(self, all_=False):
        while self.deferred:
            self.deferred.pop(0)()
            if not all_:
                break

    def norm_start(self, nd):
        self.norm_squares(0)
        self.norm_squares(1)
        self.norm_stats(0)
        self.norm_stats(1)
        self.norm_apply(nd, 0)
        self.norm_apply(nd, 1)
        self.norm_squares(2)
        self.norm_squares(3)
        for t in (2, 3):
            def st(t=t):
                self.norm_stats(t)
                self.norm_apply(nd, t)
            self.deferred.append(st)

    def phase_desc(self, b, p):
        kind, l, f = p
        n = 1 if kind == "mix" else (0 if f == 0 else 2)
        mb = self.modb(l, b)
        sc_ = self.scal(l, b)
        A_view = sc_.sub(sc_.ap[:, n * 8:(n + 1) * 8])
        sh_view = mb.sub(mb.ap[:, (3 * n) * 8:(3 * n + 1) * 8])
        return (A_view, sh_view, lambda dc, t: self.H[dc][t], None)

    def final_desc(self, b):
        AR = self.AR
        oH = 65536
        sf = self.scalfin(b)
        mf = self.modfin(b)
        sh_view = mf.sub(mf.ap[:, 0:8])

        def out_fn(dc, t):
            return AR.view(oH + (t % 2) * 16384 + dc * TW * 4, F32, TW)

        def post(t):
            sg = AR.view(oH + (t % 2) * 16384, F32, DC * TW, (DC, TW))
            dst = self.outT[b].rearrange("(a p) t -> p a t", p=128)[:, :, t * TW:(t + 1) * TW]
            self.out_ops.append(self.dma("sp", dst, sg.ap, reads=[sg]))
            if b + 1 < NB:
                self.load_x_tile(b + 1, t)
        return (sf, sh_view, out_fn, post)

    def ffn(self, b, l, f, next_nd=None):
        AR, dr, o = self.AR, self.dr, self.o
        n = 0 if f == 0 else 2
        mb = self.modb(l, b)
        sc_ = self.scal(l, b)
        A_view = sc_.sub(sc_.ap[:, n * 8:(n + 1) * 8])
        sh_view = mb.sub(mb.ap[:, (3 * n) * 8:(3 * n + 1) * 8])
        hg = sc_.sub(sc_.ap[:, 24 + n * 8:24 + (n + 1) * 8])
        abuf = [[AR.view(o["oA"] + (j * T + t * TW) * 2, BF16, TW) for t in range(TT)] for j in range(8)]
        wgu = [AR.view(o["oB"] + i * 4096, BF16, 2048, (2, DC, 128)) for i in range(3)]
        wdb = [AR.view(o["oB"] + 12288 + i * 2048, BF16, 1024, (8, 128)) for i in range(2)]
        ngu = getattr(self, "_ngu", 0)
        nwd = getattr(self, "_nwd", 0)
        fc0 = 0
        GL = FSPLIT[-1]
        wall = [AR.view(o["oB"] + 16384 + dc * GL * 256, BF16, GL * 128, (GL, 128)) for dc in range(DC)]
        for gi, grp in enumerate(FSPLIT):
            lastg = (gi == len(FSPLIT) - 1)
            wts = {}

            def load_w(j):
                nonlocal ngu
                fc = fc0 + j
                key = (b, l, f, fc)
                if key in self.pre_wgu:
                    wts[j] = self.pre_wgu.pop(key)
                    return
                w = wgu[ngu % 3]
                ngu += 1
                self.dma("pool", w.ap.rearrange("p a b c -> p (a b c)"), dr["wgu"][l, f, fc], writes=[w])
                wts[j] = w
                if lastg and j == 0:
                    for dc in range(DC):
                        self.dma("pool", wall[dc].ap,
                                 dr["wd"][l, f, fc0:fc0 + grp, dc].rearrange("a p d -> p a d"), writes=[wall[dc]])

            def block(j, t):
                w = wts[j]
                pg = self.next_ps()
                pu = self.next_ps()
                for dc in range(DC):
                    self.mm(pg, w.sub(w.ap[:, 0, dc, :]), self.H[dc][t], dc == 0, dc == DC - 1)
                for dc in range(DC):
                    self.mm(pu, w.sub(w.ap[:, 1, dc, :]), self.H[dc][t], dc == 0, dc == DC - 1)
                sg = self.rot("tmp")
                self.act(sg, pg, AF.Silu)
                self.tt("dve", abuf[j][t], sg, pu, ALU.mult)

            def after_fc(j):
                fc = fc0 + j
                self.mod_step()
                if len(self.mod_pending) > NFC - 1 - fc:
                    self.mod_step()

            j_start = 0
            if gi == 0:
                load_w(0)
                load_w(1)
                for (j, t) in ((0, 0), (0, 1), (1, 0), (1, 1), (0, 2), (1, 2)):
                    block(j, t)
                    self.flush()
                self.flush(all_=True)
                for j in (0, 1):
                    block(j, TT - 1)
                    after_fc(j)
                j_start = 2
            for j in range(j_start, grp):
                load_w(j)
                for t in range(TT):
                    block(j, t)
                after_fc(j)
            if lastg:
                assert grp == GL
                for t in range(TT):
                    for dc in range(DC):
                        w = wall[dc]
                        py = self.next_ps()
                        for j in range(grp):
                            self.mm(py, w.sub(w.ap[:, j, :]), abuf[j][t], j == 0, j == grp - 1)
                        self.stt(self.X[dc][t], py, hg.ap[:, dc:dc + 1], self.X[dc][t], ALU.mult, ALU.add,
                                 extra_reads=[hg])
                        if dc == 4:
                            self.tail_mid(next_nd, t)
                    self.tail_end(next_nd, t)
                fc0 += grp
                continue
            for dc in range(DC):
                w = wdb[nwd % 2]
                nwd += 1
                wv = w.sub(w.ap[:, 0:grp, :])
                self.dma("pool", wv.ap,
                         dr["wd"][l, f, fc0:fc0 + grp, dc].rearrange("a p d -> p a d"), writes=[w])
                for t in range(TT):
                    py = self.next_ps()
                    for j in range(grp):
                        self.mm(py, w.sub(w.ap[:, j, :]), abuf[j][t], j == 0, j == grp - 1)
                    self.stt(self.X[dc][t], py, hg.ap[:, dc:dc + 1], self.X[dc][t], ALU.mult, ALU.add,
                             extra_reads=[hg])
            fc0 += grp
        self._ngu, self._nwd = ngu, nwd

    def mixer(self, b, l, next_nd=None):
        AR, dr, o, vcol = self.AR, self.dr, self.o, self.vcol
        oA, oB, oZ, oF = o["oA"], o["oB"], o["oZ"], o["oF"]
        mb = self.modb(l, b)
        sc_ = self.scal(l, b)
        g2 = sc_.sub(sc_.ap[:, 32:40])
        H, X = self.H, self.X

        VP = 2080
        vp = [AR.view(oA + cc * VP * 2, BF16, VP) for cc in range(4)]
        oD = oA + 4 * VP * 2
        oD = (oD + 255) // 256 * 256
        Dm = [AR.view(oD + i * 31 * 256, BF16, 31 * 128, (31, 128)) for i in range(2)]
        assert oD + 2 * 31 * 256 <= oA + 32768
        cv = [[AR.view(oB + (cc * T + t * TW) * 4, F32, TW) for t in range(TT)] for cc in range(4)]
        z = [[AR.view(oZ + (cc * T + t * TW) * 2, BF16, TW) for t in range(TT)] for cc in range(4)]
        wcf = [AR.view(oB + 24576 + i * 4096, BF16, 2048, (2, DC, 128)) for i in range(2)]
        assert 16384 + DC * FSPLIT[-1] * 256 <= 28672
        for cc in range(4):
            pad0 = vp[cc].sub(vp[cc].ap[:, 0:16])
            pad1 = vp[cc].sub(vp[cc].ap[:, VP - 16:VP])
            self.S.add("dve", lambda e, p=pad0: e.memset(p.ap, 0.0), writes=[pad0])
            self.S.add("dve", lambda e, p=pad1: e.memset(p.ap, 0.0), writes=[pad1])

        def build_D(cc):
            for k in range(31):
                dk = AR.view(oD + (cc % 2) * 31 * 256 + k * 256, BF16, 128)
                self.ts("dve", dk, self.ident, vcol("ccw", l, k * 4 + cc), None, ALU.mult,
                        extra_reads=[self.vec])
        wcf0 = AR.view(oB + 12288, BF16, 2048, (2, DC, 128))
        wconf = {}

        def load_conf(cc):
            w = wcf0 if cc == 0 else wcf[cc % 2]
            self.dma("pool", w.ap.rearrange("p a b c -> p (a b c)"), dr["wconf"][l, cc], writes=[w])
            wconf[cc] = w
        load_conf(0)
        load_conf(1)
        wft = AR.view(oZ, BF16, 4 * D, (4, D))
        self.dma("pool", wft.ap.rearrange("p a b -> p (a b)"), dr["wfT"][l], writes=[wft])
        c64, s64 = self.c64, self.s64

        def prep_w(Wo, cmat, d0, d1):
            for dcn in range(d0, d1):
                ps = self.next_ps()
                for cc in range(4):
                    lhs = wft.sub(wft.ap[:, cc, dcn * 128:(dcn + 1) * 128])
                    self.mm(ps, lhs, cmat, True, True, ps_ap=ps.ap[:, cc * 128:(cc + 1) * 128])
                wrow = AR.view(Wo + dcn * 1024, BF16, 512)
                self.S.add("dve", lambda e, o_=wrow, p_=ps: e.tensor_copy(out=o_.ap, in_=p_.ap),
                           reads=[ps], writes=[wrow])
        build_D(0)
        build_D(1)

        def proj(cc, t):
            w = wconf[cc]
            pa = self.next_ps()
            pb = self.next_ps()
            for dc in range(DC):
                self.mm(pa, w.sub(w.ap[:, 0, dc, :]), H[dc][t], dc == 0, dc == DC - 1)
            for dc in range(DC):
                self.mm(pb, w.sub(w.ap[:, 1, dc, :]), H[dc][t], dc == 0, dc == DC - 1)
            sg = self.rot("tmp")
            self.act(sg, pb, AF.Sigmoid)
            vt = AR.view(oA + (cc * VP + 16 + t * TW) * 2, BF16, TW)
            self.tt("dve", vt, sg, pa, ALU.mult)

        def conv(cc):
            for t in range(TT):
                pc = self.next_ps()
                for k in range(31):
                    dk = AR.view(oD + (cc % 2) * 31 * 256 + k * 256, BF16, 128)
                    rhs = AR.view(oA + (cc * VP + t * TW + k + 1) * 2, BF16, TW)
                    self.mm(pc, dk, rhs, k == 0, k == 30)
                self.act(cv[cc][t], pc, AF.Identity, bias=vcol("ccb", l, cc), extra_reads=[self.vec])

        for (cc, t) in ((0, 0), (0, 1), (1, 0), (1, 1), (0, 2), (1, 2)):
            proj(cc, t)
            self.flush()
        self.flush(all_=True)
        prep_w(oF, c64, 0, DC)
        proj(0, TT - 1)
        proj(1, TT - 1)
        conv(0)
        load_conf(2)
        conv(1)
        for t in range(TT):
            proj(2, t)
        build_D(2)
        load_conf(3)
        conv(2)
        for t in range(TT):
            proj(3, t)
        build_D(3)
        conv(3)
        lnr = [[AR.view(oF + 8192 + i * 1024, BF16, TW) for i in range(8)],
               [AR.view(oA + 12288 + i * 1024, BF16, TW) for i in range(4)] +
               [AR.view(oA + 28672 + i * 1024, BF16, TW) for i in range(4)]]
        lnc = [0, 0]

        def ln_tmp():
            half = lnc[1]
            v = lnr[half][lnc[0] % 8]
            lnc[0] += 1
            return v

        def ln_act(t):
            lnc[0], lnc[1] = 0, t % 2
            tm_ = []
            for cc in range(4):
                cb = ln_tmp()
                self.S.add("pool", lambda e, o_=cb, i_=cv[cc][t]: e.tensor_copy(out=o_.ap, in_=i_.ap),
                           reads=[cv[cc][t]], writes=[cb])
                tm_.append(cb)
            for cc in range(4):
                cq = ln_tmp()
                self.act(cq, cv[cc][t], AF.Square)
                tm_.append(cq)
            return tm_

        def ln_pe(tm_):
            p1 = self.next_ps()
            p2 = self.next_ps()
            for cc in range(4):
                self.mm(p1, self.ones, tm_[cc], cc == 0, cc == 3)
            for cc in range(4):
                self.mm(p2, self.ones, tm_[4 + cc], cc == 0, cc == 3)
            return p1, p2

        def ln_b(t, p1, p2):
            m = self.rot("rs")
            self.ts("dve", m, p1, 1.0 / 512, None, ALU.mult)
            var = self.rot("rs")
            self.tt("dve", var, m, m, ALU.mult)
            self.stt(var, p2, 1.0 / 512, var, ALU.mult, ALU.subtract)
            self.ts("dve", var, var, 0.0, EPS, ALU.max, ALU.add)
            self.act(var, var, AF.Sqrt)
            self.S.add("dve", lambda e, r=var: e.reciprocal(out=r.ap, in_=r.ap), reads=[var], writes=[var])
            for cc in range(4):
                tm = self.rot("tmp")
                self.tt("dve", tm, cv[cc][t], m, ALU.subtract)
                self.tt("dve", tm, tm, var, ALU.mult)
                self.act(z[cc][t], tm, AF.Silu, bias=vcol("lnb", l, cc), scale=vcol("lng", l, cc),
                         extra_reads=[self.vec])

        uc = [AR.view(oA + sc * 1024, BF16, 512) for sc in range(16)]
        us = [AR.view(oA + 16384 + sc * 1024, BF16, 512) for sc in range(16)]

        def g3(v, c0, n, rev=False):
            a_ = v.ap
            ps_ = a_.ap[0]
            if rev:
                return AP(tensor=a_.tensor, offset=a_.offset + c0 + n - 1, ap=[[ps_[0], ps_[1]], [64, 8], [-1, n]])
            return AP(tensor=a_.tensor, offset=a_.offset + c0, ap=[[ps_[0], ps_[1]], [64, 8], [1, n]])

        usall = AR.view(oA + 16384, BF16, 16 * 512)
        for c0 in (0, 32):
            a_ = usall.ap
            zap = AP(tensor=a_.tensor, offset=a_.offset + c0, ap=[[a_.ap[0][0], a_.ap[0][1]], [64, 128], [1, 1]])
            self.S.add("dve", lambda e, z_=zap: e.memset(z_, 0.0), writes=[usall])

        def uproj(sc0, sc1):
            for sc in range(sc0, sc1):
                t, q = sc // 4, sc % 4
                ps = self.next_ps()
                for dc in range(DC):
                    lhs = H[dc][t].sub(H[dc][t].ap[:, q * 128:(q + 1) * 128])
                    rhs = AR.view(oF + dc * 1024, BF16, 512)
                    self.mm(ps, lhs, rhs, dc == 0, dc == DC - 1)
                self.S.add("act", lambda e, o_=g3(uc[sc], 0, 33), i_=g3(ps, 0, 33):
                           e.activation(out=o_, in_=i_, func=AF.Copy), reads=[ps], writes=[uc[sc]])
                self.S.add("dve", lambda e, o_=g3(uc[sc], 33, 31, rev=True), i_=g3(ps, 1, 31):
                           e.tensor_copy(out=o_, in_=i_), reads=[ps], writes=[uc[sc]])
                self.S.add("dve", lambda e, o_=g3(us[sc], 1, 31), i_=g3(ps, 33, 31):
                           e.tensor_copy(out=o_, in_=i_), reads=[ps], writes=[us[sc]])
                self.S.add("act", lambda e, o_=g3(us[sc], 33, 31, rev=True), i_=g3(ps, 33, 31):
                           e.activation(out=o_, in_=i_, func=AF.Copy, scale=-1.0), reads=[ps], writes=[us[sc]])

        fill = []
        for sc0 in range(0, 16, 2):
            fill.append(lambda sc0=sc0: uproj(sc0, sc0 + 2))
        ta = ln_act(0)
        fill.pop(0)()
        pa = ln_pe(ta)
        for t in range(TT):
            if t + 1 < TT:
                ta = ln_act(t + 1)
            fill.pop(0)()
            ln_b(t, *pa)
            if t + 1 < TT:
                pa = ln_pe(ta)
        fill.pop(0)()
        uslast = AR.view(oA + 16384 + 12 * 1024, BF16, 4 * 512)
        for c0 in (0, 32):
            a_ = uslast.ap
            zap = AP(tensor=a_.tensor, offset=a_.offset + c0, ap=[[a_.ap[0][0], a_.ap[0][1]], [64, 32], [1, 1]])
            self.S.add("dve", lambda e, z_=zap: e.memset(z_, 0.0), writes=[uslast])
        while fill:
            fill.pop(0)()
        csb = [AR.view(oB + i * 16384, BF16, 2 * 16 * KTW, (2, 16, KTW)) for i in range(2)]
        for kt in range(NKT):
            cb = csb[kt % 2]
            self.dma("sp", cb.ap.rearrange("p a b c -> p (a b c)"), dr["cs"][kt], writes=[cb])
            for cc in range(4):
                pe_ = self.next_ps()
                po_ = self.next_ps()
                pev = pe_.ap[:, 0:KTW]
                pov = po_.ap[:, 0:KTW]
                for sc in range(16):
                    self.mm(pe_, uc[sc].sub(uc[sc].ap[:, cc * 128:(cc + 1) * 128]),
                            cb.sub(cb.ap[:, 0, sc, :]), sc == 0, sc == 15, ps_ap=pev)
                for sc in range(16):
                    self.mm(po_, us[sc].sub(us[sc].ap[:, cc * 128:(cc + 1) * 128]),
                            cb.sub(cb.ap[:, 1, sc, :]), sc == 0, sc == 15, ps_ap=pov)
                esb = self.rot("tmp")
                ev = esb.sub(esb.ap[:, 0:KTW])
                self.act(ev, pe_.sub(pev), AF.Copy)
                fo = AR.view(oF + (cc * T + kt * KTW) * 2, BF16, KTW)
                self.tt("dve", fo, ev, po_.sub(pov), ALU.add)
                j0 = 1 if kt == 0 else 0
                n = KTW - j0
                hi = T - kt * KTW - j0
                fh = AR.view(oF + (cc * T + hi - (n - 1)) * 2, BF16, n)
                a_ = fh.ap
                rap = AP(tensor=a_.tensor, offset=a_.offset + (n - 1), ap=[[a_.ap[0][0], a_.ap[0][1]], [-1, n]])
                self.tt("dve", View(rap, fh.keys), esb.sub(esb.ap[:, j0:KTW]), po_.sub(po_.ap[:, j0:KTW]),
                        ALU.subtract)
        for cc in range(4):
            pn = self.next_ps()
            for sc in range(16):
                self.mm(pn, uc[sc].sub(uc[sc].ap[:, cc * 128:(cc + 1) * 128]),
                        self.csn.sub(self.csn.ap[:, sc:sc + 1]), sc == 0, sc == 15, ps_ap=pn.ap[:, 0:1])
            fn_ = AR.view(oF + (cc * T + T // 2) * 2, BF16, 1)
            self.act(fn_, pn.sub(pn.ap[:, 0:1]), AF.Copy)

        mg = [[AR.view(oA + (dc * T + t * TW) * 2, BF16, TW) for t in range(TT)] for dc in range(DC)]
        wmgb = [AR.view(oB + 16640 + i * 3072, BF16, 12 * 128, (12, 128)) for i in range(3)]
        nmg = getattr(self, "_nmg", 0)

        def merge(i, src, first, after_dma2=None):
            nonlocal nmg
            for dc in range(DC):
                w = wmgb[nmg % 3]
                nmg += 1
                self.dma("pool", w.ap.rearrange("p a b -> p (a b)"), dr["wmg"][l, i, dc], writes=[w])
                if dc == 2 and after_dma2 is not None:
                    after_dma2()
                for t in range(TT):
                    pg = self.next_ps()
                    py = self.next_ps()
                    for di in range(DC):
                        self.mm(pg, w.sub(w.ap[:, di, :]), H[di][t], di == 0, di == DC - 1)
                    for cc in range(4):
                        self.mm(py, w.sub(w.ap[:, 8 + cc, :]), src(cc, t), cc == 0, cc == 3)
                    sg = self.rot("tmp")
                    self.act(sg, pg, AF.Sigmoid, bias=vcol("bgate", l, i * 8 + dc), extra_reads=[self.vec])
                    if first:
                        self.tt("dve", mg[dc][t], sg, py, ALU.mult)
                    else:
                        self.tt("dve", sg, sg, py, ALU.mult)
                        self.tt("dve", mg[dc][t], mg[dc][t], sg, ALU.add)

        merge(0, lambda cc, t: AR.view(oF + (cc * T + t * TW) * 2, BF16, TW), True)
        merge(2, lambda cc, t: z[cc][t], False)

        PP = 2052
        pp = [AR.view(oB + cc * PP * 2, BF16, PP) for cc in range(4)]
        assert 4 * PP * 2 <= 16640
        r = [[AR.view(oZ + (cc * T + t * TW) * 2, BF16, TW) for t in range(TT)] for cc in range(4)]
        d3 = [AR.view(oF + i * 256, BF16, 128) for i in range(12)]
        wsh = [AR.view(oF + 3072 + i * 6144, BF16, 3 * DC * 128, (3, DC, 128)) for i in range(2)]
        for cc in range(4):
            pad0 = pp[cc].sub(pp[cc].ap[:, 0:2])
            pad1 = pp[cc].sub(pp[cc].ap[:, PP - 2:PP])
            self.S.add("dve", lambda e, p=pad0: e.memset(p.ap, 0.0), writes=[pad0])
            self.S.add("dve", lambda e, p=pad1: e.memset(p.ap, 0.0), writes=[pad1])
            for k in range(3):
                self.ts("dve", d3[k * 4 + cc], self.ident, vcol("csw", l, k * 4 + cc), None, ALU.mult,
                        extra_reads=[self.vec])
        for cc in range(4):
            w = wsh[cc % 2]
            self.dma("pool", w.ap.rearrange("p a b c -> p (a b c)"), dr["wshort"][l, cc], writes=[w])
            for t in range(TT):
                p1 = self.next_ps()
                p2 = self.next_ps()
                for dc in range(DC):
                    self.mm(p1, w.sub(w.ap[:, 0, dc, :]), H[dc][t], dc == 0, dc == DC - 1)
                for dc in range(DC):
                    self.mm(p2, w.sub(w.ap[:, 1, dc, :]), H[dc][t], dc == 0, dc == DC - 1)
                tm = self.rot("tmp")
                self.act(tm, p1, AF.Copy)
                pt = AR.view(oB + (cc * PP + 2 + t * TW) * 2, BF16, TW)
                self.tt("dve", pt, tm, p2, ALU.mult)
            for t in range(TT):
                pq = self.next_ps()
                pbg = self.next_ps()
                for k in range(3):
                    rhs = AR.view(oB + (cc * PP + t * TW + k + 1) * 2, BF16, TW)
                    self.mm(pq, d3[k * 4 + cc], rhs, k == 0, k == 2)
                for dc in range(DC):
                    self.mm(pbg, w.sub(w.ap[:, 2, dc, :]), H[dc][t], dc == 0, dc == DC - 1)
                tm = self.rot("tmp")
                self.act(tm, pq, AF.Copy)
                self.tt("dve", r[cc][t], tm, pbg, ALU.mult)
        woall = [AR.view(oF + i * 2048, BF16, DC * 128, (DC, 128)) for i in range(DC)]

        def load_wo():
            for dcn in range(DC):
                self.dma("pool", woall[dcn].ap.rearrange("p a b -> p (a b)"), dr["wo"][l, dcn], writes=[woall[dcn]])
        merge(1, lambda cc, t: r[cc][t], False, after_dma2=load_wo)
        self._nmg = nmg

        for t in range(TT):
            for dcn in range(DC):
                w = woall[dcn]
                po = self.next_ps()
                for dc in range(DC):
                    self.mm(po, w.sub(w.ap[:, dc, :]), mg[dc][t], dc == 0, dc == DC - 1)
                self.stt(X[dcn][t], po, g2.ap[:, dcn:dcn + 1], X[dcn][t], ALU.mult, ALU.add,
                         extra_reads=[g2])
                if dcn == 4:
                    self.tail_mid(next_nd, t)
            self.tail_end(next_nd, t)

    def finish(self, b, outT):
        AR, o = self.AR, self.o
        ops = []
        oH = 65536
        stg = [AR.view(oH + i * 16384, F32, DC * TW, (DC, TW)) for i in range(2)]
        if self.final:
            sf = self.scalfin(b)
            mf = self.modfin(b)
        for t in range(TT):
            sg = stg[t % 2]
            if self.final:
                r = self.rstd_tile(t)
            for dc in range(DC):
                ot = AR.view(oH + (t % 2) * 16384 + dc * TW * 4, F32, TW)
                if self.final:
                    tm = self.rot("tmp")
                    self.stt(tm, self.X[dc][t], sf.ap[:, dc:dc + 1], r, ALU.mult, ALU.mult, extra_reads=[sf])
                    self.act(ot, tm, AF.Identity, bias=mf.ap[:, dc:dc + 1], extra_reads=[mf])
                else:
                    self.act(ot, self.X[dc][t], AF.Copy)
            dst = outT[b].rearrange("(a p) t -> p a t", p=128)[:, :, t * TW:(t + 1) * TW]
            ops.append(self.dma("sp", dst, sg.ap, reads=[sg]))
        return ops


def _consts():
    s = np.arange(T, dtype=np.float64)
    ang = 2.0 * np.pi * np.outer(s, s) / T
    C = np.cos(ang) / np.sqrt(T)
    Sn = -np.sin(ang) / np.sqrt(T)
    cs = np.stack([C, Sn], 0)[:, :, :NKT * KTW].reshape(2, 16, 128, NKT, KTW).transpose(3, 2, 0, 1, 4)
    cs = np.ascontiguousarray(cs).reshape(NKT, 128, 2 * 16 * KTW).astype(ml_dtypes.bfloat16)
    nyq = np.zeros((128, 128))
    nyq[:, 0:16] = C[:, T // 2].reshape(16, 128).T
    j = np.arange(64, dtype=np.float64)
    a64 = 2.0 * np.pi * np.outer(j, j) / 64
    c64 = np.zeros((128, 128))
    s64 = np.zeros((128, 128))
    for g in range(2):
        c64[g * 64:(g + 1) * 64, g * 64:(g + 1) * 64] = np.cos(a64) / 8.0
        s64[g * 64:(g + 1) * 64, g * 64:(g + 1) * 64] = np.sin(a64) / 8.0
    sel = np.zeros((128, 128))
    for g in range(2):
        sel[:, g * 64:g * 64 + 33] = c64[:, g * 64:g * 64 + 33]
        sel[:, g * 64 + 33:g * 64 + 64] = s64[:, g * 64 + 1:g * 64 + 32]
    cst = np.stack([np.eye(128), np.ones((128, 128)), sel, s64, nyq], 1).astype(ml_dtypes.bfloat16)
    return cs, np.ascontiguousarray(cst)


def _col(v):
    return np.ascontiguousarray(np.asarray(v, np.float32).reshape(-1, 128).T)


def _prep_shared(inp):
    f = lambda k: np.asarray(inp[k], np.float32)
    sh = {}
    vec = np.zeros((128, NVEC), np.float32)
    for l in range(DEPTH):
        def put(name, arr):
            o = VOFF[(name, l)]
            vec[:, o:o + arr.shape[1]] = arr
        put("g0", _col(f("ffn1_norm_g")[l]))
        put("g1", _col(f("mix_norm_g")[l]))
        put("g2", _col(f("ffn2_norm_g")[l]))
        put("bmod", _col(f("b_mod")[l]))
        csw = f("conv_short_w")[l].reshape(3, 4, 128).transpose(2, 0, 1).reshape(128, 12)
        put("csw", csw)
        ccw = f("conv_conf_w")[l].reshape(31, 4, 128).transpose(2, 0, 1).reshape(128, 124)
        put("ccw", ccw)
        put("ccb", _col(f("conv_conf_b")[l]))
        put("lng", _col(f("conf_ln_g")[l]))
        put("lnb", _col(f("conf_ln_b")[l]))
        put("bgate", _col(f("b_gate")[l]))
    o = VOFF[("gf", 0)]
    vec[:, o:o + 8] = _col(f("final_norm_g"))
    o = VOFF[("bfin", 0)]
    vec[:, o:o + 16] = _col(f("b_final_mod"))
    o = VOFF[("eye2", 0)]
    vec[0, o] = 1.0
    vec[1, o + 1] = 1.0
    sh["vec"] = vec
    cs, cst = _consts()
    sh["cs"] = cs
    sh["cst"] = cst
    wm = f("w_mod").reshape(DEPTH, DC, 128, 18, 512).transpose(0, 3, 2, 1, 4).reshape(DEPTH * 18, 128, DC * 512)
    wf = f("w_final_mod").reshape(DC, 128, 4, 512).transpose(2, 1, 0, 3).reshape(4, 128, DC * 512)
    sh["wmodt"] = np.ascontiguousarray(np.concatenate([wm, wf], 0))
    def gu(k):
        return f(k).reshape(DEPTH, DC, 128, NFC, 128).transpose(0, 3, 2, 1, 4)
    wgu = np.stack([np.stack([gu("ffn1_w_gate"), gu("ffn1_w_up")], 3),
                    np.stack([gu("ffn2_w_gate"), gu("ffn2_w_up")], 3)], 1)
    sh["wgu"] = np.ascontiguousarray(wgu).reshape(DEPTH, 2, NFC, 128, 2 * DC * 128)
    def dn(k):
        return f(k).reshape(DEPTH, NFC, 128, DC, 128).transpose(0, 1, 3, 2, 4)
    sh["wd"] = np.ascontiguousarray(np.stack([dn("ffn1_w_down"), dn("ffn2_w_down")], 1))
    w_in = f("w_in")
    def colgrp(c0):
        return w_in[:, :, c0:c0 + 512].reshape(DEPTH, DC, 128, 4, 128).transpose(0, 3, 2, 1, 4)
    u_bg, u_cg, u_x, u_ga, u_gb = (colgrp(512), colgrp(1024), colgrp(1536), colgrp(2048), colgrp(2560))
    sh["wconf"] = np.ascontiguousarray(np.stack([u_ga, u_gb], 3)).reshape(DEPTH, 4, 128, 2 * DC * 128)
    sh["wshort"] = np.ascontiguousarray(np.stack([u_cg, u_x, u_bg], 3)).reshape(DEPTH, 4, 128, 3 * DC * 128)
    wfT = w_in[:, :, 0:512].transpose(0, 2, 1).reshape(DEPTH, 4, 128, D).transpose(0, 2, 1, 3)
    sh["wfT"] = np.ascontiguousarray(wfT).reshape(DEPTH, 128, 4 * D)
    wg = f("w_gate").reshape(DEPTH, DC, 128, 3, DC, 128).transpose(0, 3, 4, 2, 1, 5)
    br = np.stack([f("w_branch_f"), f("w_branch_s"), f("w_branch_c")], 1)
    br = br.reshape(DEPTH, 3, 4, 128, DC, 128).transpose(0, 1, 4, 3, 2, 5)
    sh["wmg"] = np.ascontiguousarray(np.concatenate([wg, br], 4)).reshape(DEPTH, 3, DC, 128, 12 * 128)
    wo = f("w_out").reshape(DEPTH, DC, 128, DC, 128).transpose(0, 3, 2, 1, 4)
    sh["wo"] = np.ascontiguousarray(wo).reshape(DEPTH, DC, 128, DC * 128)
    return sh


_NC_CACHE = {}


def _get_nc(nstage=3 * DEPTH, final=True):
    key = (nstage, final)
    if key not in _NC_CACHE:
        _NC_CACHE[key] = Builder(nstage, final).build()
    return _NC_CACHE[key]


def kernel(_nstage=3 * DEPTH, _final=True, _ncores=8, **inp):
    x = np.asarray(inp["x"], np.float32)
    c = np.asarray(inp["c"], np.float32)
    sh = _prep_shared(inp)
    nc = _get_nc(_nstage, _final)
    in_maps = []
    for i in range(_ncores):
        xb = x[i * NB:(i + 1) * NB]
        m = dict(sh)
        m["xT"] = np.ascontiguousarray(xb.transpose(0, 2, 1))
        cb = c[i * NB:(i + 1) * NB]
        m["cT"] = np.ascontiguousarray(cb.reshape(NB, DC, 128).transpose(2, 1, 0))
        in_maps.append(m)
    res = run_bass_kernel_spmd(nc, in_maps, core_ids=list(range(_ncores)))
    outs = [np.asarray(r["outT"]).transpose(0, 2, 1) for r in res.results]
    return np.ascontiguousarray(np.concatenate(outs, 0)).astype(np.float32)
```
